# Optimizing a Trainium2 kernel written in Bass

```python
import math
import jax, jax.numpy as jnp
from jax import lax
import numpy as np

D_MODEL = 1024
BATCH = 16
SEQ = 4096
DEPTH = 1

SSM_EXPAND = 2
SSM_D_INNER = SSM_EXPAND * D_MODEL
SSM_HEAD_DIM = 64
SSM_N_HEADS = SSM_D_INNER // SSM_HEAD_DIM
SSM_N_GROUPS = 4
SSM_D_STATE = 128
SSM_CONV = 4
SSM_CHUNK = 128
SSM_CONV_DIM = SSM_D_INNER + 2 * SSM_N_GROUPS * SSM_D_STATE

ATT_HEAD_DIM = 128
ATT_HEADS_PER_GROUP = 4
ATT_PATTERNS = ((128, 1), (512, 4), (2048, 16))
ATT_N_HEADS = ATT_HEADS_PER_GROUP * len(ATT_PATTERNS)
ATT_QKV_DIM = 3 * ATT_N_HEADS * ATT_HEAD_DIM
ATT_OUT_DIM = ATT_HEADS_PER_GROUP * ATT_HEAD_DIM
ATT_BLOCK = 128
ROPE_THETA = 10000.0

N_BRANCH = 2
D_FF = -(-8 * D_MODEL // (3 * 256)) * 256
IN_PROJ_SIZES = (SSM_D_INNER, SSM_CONV_DIM, SSM_N_HEADS, ATT_QKV_DIM, N_BRANCH * D_MODEL)
IN_PROJ_DIM = sum(IN_PROJ_SIZES)
EPS = 1e-6

kernel_name = "hybrid_ssd_dilated_swa_block"


def rmsnorm(x, g):
    xf = x.astype(jnp.float32)
    y = xf * lax.rsqrt(jnp.mean(xf * xf, axis=-1, keepdims=True) + EPS)
    return (y * g.astype(jnp.float32)).astype(x.dtype)


def rope(t, pos):
    half = t.shape[-1] // 2
    inv = ROPE_THETA ** (-jnp.arange(half, dtype=jnp.float32) / half)
    ang = pos.astype(jnp.float32)[:, None] * inv[None, :]
    cos = jnp.cos(ang)[None, :, None, :]
    sin = jnp.sin(ang)[None, :, None, :]
    t1, t2 = t[..., :half], t[..., half:]
    return jnp.concatenate([t1 * cos - t2 * sin, t2 * cos + t1 * sin], axis=-1)


def segsum(a):
    T = a.shape[-1]
    xx = jnp.broadcast_to(a[..., :, None], a.shape + (T,))
    cs = jnp.cumsum(jnp.where(jnp.tril(jnp.ones((T, T), bool), -1), xx, 0.0), axis=-2)
    return jnp.where(jnp.tril(jnp.ones((T, T), bool)), cs, -jnp.inf)


def causal_depthwise_conv(u, w, b):
    K, C = w.shape
    out = lax.conv_general_dilated(u, w[:, None, :], window_strides=(1,), padding=[(K - 1, 0)],
                                   dimension_numbers=('NWC', 'WIO', 'NWC'), feature_group_count=C)
    return out + b


def ssd_chunked(xs, dt, A, Bm, Cm):
    b, S, H, P = xs.shape
    G, N = Bm.shape[-2:]
    J = H // G
    Q = SSM_CHUNK
    nc = S // Q
    xdt = (xs * dt[..., None]).reshape(b, nc, Q, G, J, P)
    a = (dt * A).reshape(b, nc, Q, G, J).transpose(0, 1, 3, 4, 2)
    a_cs = jnp.cumsum(a, axis=-1)
    Br = Bm.reshape(b, nc, Q, G, N)
    Cr = Cm.reshape(b, nc, Q, G, N)
    tri = jnp.tril(jnp.ones((Q, Q), bool))
    Lmat = jnp.exp(jnp.where(tri, a_cs[..., :, None] - a_cs[..., None, :], -jnp.inf))
    CB = jnp.einsum('bclgn,bcsgn->bcgls', Cr, Br)
    y_diag = jnp.einsum('bcgjls,bcsgjp->bclgjp', CB[:, :, :, None] * Lmat, xdt)
    decay_states = jnp.exp(a_cs[..., -1:] - a_cs)
    states = jnp.einsum('bclgn,bcgjl,bclgjp->bcgjpn', Br, decay_states, xdt)
    chunk_tot = jnp.pad(a_cs[..., -1].transpose(0, 2, 3, 1), ((0, 0), (0, 0), (0, 0), (1, 0)))
    decay_chunk = jnp.exp(segsum(chunk_tot))
    states = jnp.concatenate([jnp.zeros_like(states[:, :1]), states], axis=1)
    new_states = jnp.einsum('bgjzc,bcgjpn->bzgjpn', decay_chunk, states)
    prev_states = new_states[:, :-1]
    y_off = jnp.einsum('bclgn,bcgjpn,bcgjl->bclgjp', Cr, prev_states, jnp.exp(a_cs))
    return (y_diag + y_off).reshape(b, S, H, P)


def mamba2_branch(z, xBC, dt_raw, conv_w, conv_b, dt_bias, a_log, d_skip, ssm_norm):
    b, S, _ = z.shape
    xBC = jax.nn.silu(causal_depthwise_conv(xBC, conv_w, conv_b)).astype(jnp.float32)
    gn = SSM_N_GROUPS * SSM_D_STATE
    xs = xBC[..., :SSM_D_INNER].reshape(b, S, SSM_N_HEADS, SSM_HEAD_DIM)
    Bm = xBC[..., SSM_D_INNER:SSM_D_INNER + gn].reshape(b, S, SSM_N_GROUPS, SSM_D_STATE)
    Cm = xBC[..., SSM_D_INNER + gn:].reshape(b, S, SSM_N_GROUPS, SSM_D_STATE)
    dt = jax.nn.softplus(dt_raw.astype(jnp.float32) + dt_bias.astype(jnp.float32))
    A = -jnp.exp(a_log.astype(jnp.float32))
    y = ssd_chunked(xs, dt, A, Bm, Cm) + d_skip.astype(jnp.float32)[:, None] * xs
    y = y.reshape(b, S, SSM_D_INNER) * jax.nn.silu(z.astype(jnp.float32))
    yg = y.reshape(b, S, SSM_N_GROUPS, SSM_D_INNER // SSM_N_GROUPS)
    yg = yg * lax.rsqrt(jnp.mean(yg * yg, axis=-1, keepdims=True) + EPS)
    y = yg.reshape(b, S, SSM_D_INNER) * ssm_norm.astype(jnp.float32)
    return y.astype(z.dtype)


def dilated_window_group(q, k, v, window, dilation):
    b, S, h, d = q.shape
    r = dilation
    w_sub = window // r
    L = S // r
    nb = -(-L // ATT_BLOCK)
    Lp = nb * ATT_BLOCK

    def to_blocks(t):
        t = t.reshape(b, L, r, h, d).transpose(0, 2, 1, 3, 4)
        t = jnp.pad(t, ((0, 0), (0, 0), (0, Lp - L), (0, 0), (0, 0)))
        return t.reshape(b, r, nb, ATT_BLOCK, h, d)

    def with_prev(t):
        prev = jnp.pad(t[:, :, :-1], ((0, 0), (0, 0), (1, 0), (0, 0), (0, 0), (0, 0)))
        return jnp.concatenate([prev, t], axis=3)

    qb = to_blocks(q)
    kk = with_prev(to_blocks(k))
    vv = with_prev(to_blocks(v))
    s = jnp.einsum('brnqhd,brnkhd->brnhqk', qb, kk) * (d ** -0.5)
    qi = jnp.arange(ATT_BLOCK)[:, None]
    kj = jnp.arange(2 * ATT_BLOCK)[None, :]
    dist = qi + ATT_BLOCK - kj
    band = (dist >= 0) & (dist <= w_sub)
    has_prev = (jnp.arange(nb) > 0)[:, None, None] | (kj >= ATT_BLOCK)[None]
    mask = band[None] & has_prev
    s = jnp.where(mask[None, None, :, None], s, -jnp.inf)
    m = jnp.max(s, axis=-1, keepdims=True)
    p = jnp.exp(s - m)
    den = jnp.sum(p, axis=-1, keepdims=True)
    o = jnp.einsum('brnhqk,brnkhd->brnqhd', p / den, vv)
    lse = (m + jnp.log(den))[..., 0].transpose(0, 1, 2, 4, 3)
    o = o.reshape(b, r, Lp, h, d)[:, :, :L].transpose(0, 2, 1, 3, 4).reshape(b, S, h, d)
    lse = lse.reshape(b, r, Lp, h)[:, :, :L].transpose(0, 2, 1, 3).reshape(b, S, h)
    return o, lse


def dilated_attention_branch(q, k, v):
    outs, lses = [], []
    for gi, (window, dilation) in enumerate(ATT_PATTERNS):
        sl = slice(gi * ATT_HEADS_PER_GROUP, (gi + 1) * ATT_HEADS_PER_GROUP)
        o, lse = dilated_window_group(q[:, :, sl], k[:, :, sl], v[:, :, sl], window, dilation)
        outs.append(o)
        lses.append(lse)
    o = jnp.stack(outs, axis=0)
    wts = jax.nn.softmax(jnp.stack(lses, axis=0), axis=0)
    return jnp.sum(wts[..., None] * o, axis=0)


def setup_inputs(seed: int = 0) -> dict:
    key = jax.random.key(seed)
    ks = jax.random.split(key, 20)
    f32 = jnp.float32

    def nrm(k, shape, scale):
        return jax.random.normal(k, shape, f32) * scale

    dt0 = jnp.exp(jax.random.uniform(ks[6], (DEPTH, SSM_N_HEADS), f32,
                                     minval=math.log(1e-3), maxval=math.log(1e-1)))
    return {
        "x": nrm(ks[0], (BATCH, SEQ, D_MODEL), 1.0),
        "norm_mix": 1.0 + nrm(ks[1], (DEPTH, D_MODEL), 0.05),
        "w_in": nrm(ks[2], (DEPTH, D_MODEL, IN_PROJ_DIM), D_MODEL ** -0.5),
        "b_gate": nrm(ks[3], (DEPTH, N_BRANCH * D_MODEL), 0.01),
        "conv_w": nrm(ks[4], (DEPTH, SSM_CONV, SSM_CONV_DIM), SSM_CONV ** -0.5),
        "conv_b": nrm(ks[5], (DEPTH, SSM_CONV_DIM), 0.01),
        "dt_bias": dt0 + jnp.log(-jnp.expm1(-dt0)),
        "a_log": jnp.log(jax.random.uniform(ks[7], (DEPTH, SSM_N_HEADS), f32, minval=1.0, maxval=16.0)),
        "d_skip": 1.0 + nrm(ks[8], (DEPTH, SSM_N_HEADS), 0.1),
        "ssm_norm": 1.0 + nrm(ks[9], (DEPTH, SSM_D_INNER), 0.05),
        "w_ssm_out": nrm(ks[10], (DEPTH, SSM_D_INNER, D_MODEL), SSM_D_INNER ** -0.5),
        "w_att_out": nrm(ks[11], (DEPTH, ATT_OUT_DIM, D_MODEL), ATT_OUT_DIM ** -0.5),
        "w_mix_out": nrm(ks[12], (DEPTH, D_MODEL, D_MODEL), D_MODEL ** -0.5),
        "norm_ffn": 1.0 + nrm(ks[13], (DEPTH, D_MODEL), 0.05),
        "w_ffn_gate": nrm(ks[14], (DEPTH, D_MODEL, D_FF), D_MODEL ** -0.5),
        "w_ffn_up": nrm(ks[15], (DEPTH, D_MODEL, D_FF), D_MODEL ** -0.5),
        "w_ffn_down": nrm(ks[16], (DEPTH, D_FF, D_MODEL), D_FF ** -0.5),
        "norm_final": 1.0 + nrm(ks[17], (D_MODEL,), 0.05),
    }


def reference(x, norm_mix, w_in, b_gate, conv_w, conv_b, dt_bias, a_log, d_skip, ssm_norm,
              w_ssm_out, w_att_out, w_mix_out, norm_ffn, w_ffn_gate, w_ffn_up, w_ffn_down, norm_final):
    b, S, _ = x.shape
    pos = jnp.arange(S)
    offs = [0]
    for sz in IN_PROJ_SIZES[:-1]:
        offs.append(offs[-1] + sz)
    for l in range(DEPTH):
        h = rmsnorm(x, norm_mix[l])
        proj = h @ w_in[l]
        z, xBC, dt_raw, qkv, gate_logits = jnp.split(proj, offs[1:], axis=-1)

        y_ssm = mamba2_branch(z, xBC, dt_raw, conv_w[l], conv_b[l], dt_bias[l], a_log[l],
                              d_skip[l], ssm_norm[l]) @ w_ssm_out[l]

        qkv = qkv.astype(jnp.float32).reshape(b, S, 3, ATT_N_HEADS, ATT_HEAD_DIM)
        q = rope(qkv[:, :, 0], pos)
        k = rope(qkv[:, :, 1], pos)
        v = qkv[:, :, 2]
        y_att = dilated_attention_branch(q, k, v).reshape(b, S, ATT_OUT_DIM).astype(x.dtype) @ w_att_out[l]

        gates = jax.nn.sigmoid((gate_logits + b_gate[l]).astype(jnp.float32))
        gates = gates.reshape(b, S, N_BRANCH, D_MODEL).astype(x.dtype)
        mixed = gates[:, :, 0] * y_ssm + gates[:, :, 1] * y_att
        x = x + mixed @ w_mix_out[l]

        h = rmsnorm(x, norm_ffn[l])
        x = x + (jax.nn.silu(h @ w_ffn_gate[l]) * (h @ w_ffn_up[l])) @ w_ffn_down[l]
    return rmsnorm(x, norm_final)
```

```python
import numpy as np
import concourse.bass as bass
import concourse.mybir as mybir
from concourse.bass_utils import run_bass_kernel_spmd

F32 = mybir.dt.float32
BF16 = mybir.dt.bfloat16
AF = mybir.ActivationFunctionType
ALU = mybir.AluOpType
AX = mybir.AxisListType

S = 4096
D = 1024
NB = 2
TC = 2
T = TC * 128
NMT = S // T
DI = 2048
CONVD = 3072
DFF = 2816
INP = 11808
OFF_Z, OFF_XBC, OFF_DT, OFF_QKV, OFF_G = 0, 2048, 5120, 5152, 9760
EPS = 1e-6
NEG = -30000.0
PATTERNS = ((128, 1), (512, 4), (2048, 16))
AUW = 520
W_EL = 4096
ENGS = ['pe', 'act', 'dve', 'pool', 'sp']


class Tok:
    __slots__ = ('name', 'w', 'r', 'excl')

    def __init__(self, name):
        self.name = name
        self.w = None
        self.r = []
        self.excl = False


class Op:
    __slots__ = ('eng', 'idx', 'fn', 'waits', 'signal', 'sigval', 'dma', 'sem', 'semval')

    def __init__(self, eng, idx, fn, dma):
        self.eng = eng
        self.idx = idx
        self.fn = fn
        self.waits = []
        self.signal = False
        self.sigval = None
        self.dma = dma
        self.sem = None
        self.semval = None


class Rec:
    def __init__(self):
        self.call = None

    def __getattr__(self, name):
        def f(*a, **k):
            self.call = (name, a, k)
            return self
        return f


class Prog:
    def __init__(self, nc):
        self.nc = nc
        self.ops = {e: [] for e in ENGS}
        self.seen = {e: {} for e in ENGS}
        self.seen_dma = {e: set() for e in ENGS}
        self.cnt = {e: nc.alloc_semaphore(name=f"cnt_{e}") for e in ['pe', 'act', 'dve', 'pool']}
        self.dma_pool = {q: [nc.alloc_semaphore(name=f"dma_{q}_{i}") for i in range(n)]
                         for q, n in (('sp', 12), ('act', 4), ('pool', 16))}
        self.dma_count = {q: 0 for q in self.dma_pool}
        self.dma_last = {q: {} for q in self.dma_pool}
        self.ntok = 0

    def tok(self, name=None):
        self.ntok += 1
        return Tok(name or f"t{self.ntok}")

    def add(self, eng, fn, reads=(), writes=(), dma=False):
        ops = self.ops[eng]
        rec = Rec()
        fn(rec)
        op = Op(eng, len(ops), rec.call, dma)
        deps = {}
        for t in reads:
            if t.w is not None:
                deps[id(t.w)] = (t.w, True)
            if t.excl:
                for r in t.r:
                    if r.eng != eng and id(r) not in deps:
                        deps[id(r)] = (r, False)
        for t in writes:
            if t.w is not None and id(t.w) not in deps:
                deps[id(t.w)] = (t.w, False)
            for r in t.r:
                if id(r) not in deps:
                    deps[id(r)] = (r, False)
        seen = self.seen[eng]
        best = {}
        for d, raw in deps.values():
            if d.dma:
                if id(d) in self.seen_dma[eng]:
                    continue
                self.seen_dma[eng].add(id(d))
                op.waits.append(d)
            else:
                if d.eng == eng and not dma:
                    if eng == 'pe' or not raw:
                        continue
                if seen.get(d.eng, -1) >= d.idx:
                    continue
                if d.eng not in best or best[d.eng].idx < d.idx:
                    best[d.eng] = d
        for d in best.values():
            seen[d.eng] = d.idx
            d.signal = True
            op.waits.append(d)
        if dma:
            q = eng
            k = self.dma_count[q]
            self.dma_count[q] = k + 1
            pool = self.dma_pool[q]
            si = k % len(pool)
            op.sem = pool[si]
            op.semval = 16 * (k // len(pool) + 1)
            prev = self.dma_last[q].get(si)
            if prev is not None and id(prev) not in self.seen_dma[eng]:
                self.seen_dma[eng].add(id(prev))
                op.waits.append(prev)
            self.dma_last[q][si] = op
        ops.append(op)
        for t in reads:
            t.r.append(op)
        for t in writes:
            t.w = op
            t.r = []
        return op

    def pe(self, fn, reads=(), writes=()):
        return self.add('pe', fn, reads, writes)

    def act(self, fn, reads=(), writes=()):
        return self.add('act', fn, reads, writes)

    def dve(self, fn, reads=(), writes=()):
        return self.add('dve', fn, reads, writes)

    def pool(self, fn, reads=(), writes=()):
        return self.add('pool', fn, reads, writes)

    def dma(self, fn, reads=(), writes=(), q='sp'):
        return self.add(q, fn, reads, writes, dma=True)

    def emit(self, final_waits=()):
        nc = self.nc
        for e in ['pe', 'act', 'dve', 'pool']:
            c = 0
            for op in self.ops[e]:
                if op.dma:
                    continue
                if op.signal:
                    c += 1
                    op.sigval = c
        cnt = self.cnt

        def run(engname, engobj, extra=()):
            for op in self.ops[engname]:
                for d in op.waits:
                    if d.dma:
                        engobj.wait_ge(d.sem, d.semval)
                    else:
                        engobj.wait_ge(cnt[d.eng], d.sigval)
                name, a, k = op.fn
                ins = getattr(engobj, name)(*a, **k)
                if op.dma:
                    ins.then_inc(op.sem, 16)
                elif op.signal:
                    ins.then_inc(cnt[op.eng], 1)
            for d in extra:
                engobj.wait_ge(d.sem, d.semval)

        with nc.Block() as block:
            @block.tensor
            def _(e):
                run('pe', e)

            @block.scalar
            def _(e):
                run('act', e)

            @block.vector
            def _(e):
                run('dve', e)

            @block.gpsimd
            def _(e):
                run('pool', e, extra=[d for d in final_waits if d.eng == 'pool'])

            @block.sync
            def _(e):
                run('sp', e, extra=[d for d in final_waits if d.eng == 'sp'])


class Buf:
    def __init__(self, P, t, name):
        self.t = t
        self.k = P.tok(name)

    def __getitem__(self, key):
        return self.t[key]


class Ring:
    def __init__(self, bufs):
        self.bufs = bufs
        self.i = 0

    def next(self):
        b = self.bufs[self.i % len(self.bufs)]
        self.i += 1
        return b


def build_nc(cfg=None):
    cfg = cfg or {}
    STAGE = cfg.get('stage', 99)
    nc = bass.Bass("TRN2", target_bir_lowering=False)
    P = Prog(nc)

    def dram_in(name, shape):
        return nc.dram_tensor(name, list(shape), F32, kind="ExternalInput")

    x_d = dram_in("x", [NB, S, D])
    win_d = dram_in("w_in", [D, INP])
    wso_d = dram_in("w_ssm_out", [DI, D])
    wao_d = dram_in("w_att_out", [512, D])
    wmx_d = dram_in("w_mix_out", [D, D])
    wg_d = dram_in("w_ffn_gate", [D, DFF])
    wu_d = dram_in("w_ffn_up", [D, DFF])
    wd_d = dram_in("w_ffn_down", [DFF, D])
    c_mats = dram_in("c_mats", [128, 4 * 128 + 256])
    c_ident = dram_in("c_ident", [128, 128])
    c_wdiag = dram_in("c_wdiag", [128, 4 * 24 * 128])
    c_convb_row = dram_in("c_convb_row", [1, CONVD])
    c_pp = dram_in("c_pp", [128, 72])
    c_bc = dram_in("c_bc", [128, 96])
    c_nf = dram_in("c_nf", [128, D])
    rope_d = dram_in("rope", [3, 4, 128, S])
    out_d = nc.dram_tensor("out", [NB, S, D], F32, kind="ExternalOutput")
    au_d = nc.dram_tensor("au", [NB, 2, S, AUW], F32, kind="Internal")

    def dram_bf(name, shape):
        return nc.dram_tensor(name, list(shape), BF16, kind="Internal")


    def sb(name, shape, dt=F32):
        return Buf(P, nc.alloc_sbuf_tensor(name, list(shape), dt), name)

    mats_f = sb("mats_f", [128, 4 * 128 + 256])
    ident_f = sb("ident_f", [128, 128])
    ident_b = sb("ident_b", [128, 128], BF16)
    rot_b = sb("rot_b", [128, 128], BF16)
    maskb_b = sb("maskb_b", [128, 256], BF16)
    wdiag_b = sb("wdiag_b", [128, 4, 24, 128], BF16)
    convb_row_b = sb("convb_row_b", [1, CONVD], BF16)
    ones_row_b = sb("ones_row_b", [1, 128], BF16)
    pp = sb("pp", [128, 72])
    bc = sb("bc", [128, 96])
    nf = sb("nf", [128, D])
    A_bc = sb("A_bc", [128, 32])
    tri_f = mats_f[:, 0:128]
    ustrict_f = mats_f[:, 128:256]
    ones_f = mats_f[:, 256:384]
    PP_CONVB, PP_BG, PP_SSMN, PP_NMIX, PP_NFFN = 0, 24, 40, 56, 64

    wsrc = {"win": win_d, "wso": wso_d, "wao": wao_d, "wmx": wmx_d, "wg": wg_d, "wu": wu_d, "wd": wd_d}
    panel_cache = {}
    panel_order = []

    def get_panel(name, K, f0, ncols, r0=0):
        key = (name, K, f0, ncols, r0)
        if key not in panel_cache:
            wbp = nc.dram_tensor(f"wp_{len(panel_cache)}", [128, K * ncols], BF16, kind="Internal")
            tk = P.tok(f"wp_{len(panel_cache)}")
            src = wsrc[name]
            P.dma(lambda e: e.dma_start(out=wbp.ap().rearrange("p (k f) -> p k f", k=K),
                                        in_=src.ap()[r0:r0 + K * 128, f0:f0 + ncols].rearrange("(k p) f -> p k f", p=128)),
                  writes=[tk], q='pool')
            panel_cache[key] = (wbp, tk)
            panel_order.append(key)
        return panel_cache[key]

    P.dma(lambda e: e.dma_start(out=mats_f[:], in_=c_mats.ap()), writes=[mats_f.k])
    P.dma(lambda e: e.dma_start(out=ident_f[:], in_=c_ident.ap()), writes=[ident_f.k])
    P.dma(lambda e: e.dma_start(out=pp[:], in_=c_pp.ap()), writes=[pp.k])
    P.dma(lambda e: e.dma_start(out=bc[:], in_=c_bc.ap()), writes=[bc.k])
    P.dma(lambda e: e.dma_start(out=nf[:], in_=c_nf.ap()), writes=[nf.k])
    if cfg.get('misc', 3) & 1:
        P.dma(lambda e: e.dma_start(out=wdiag_b[:], in_=c_wdiag.ap().rearrange("p (j c f) -> p j c f", j=4, c=24)),
              writes=[wdiag_b.k], q='pool')
    if cfg.get('misc', 3) & 2:
        P.dma(lambda e: e.dma_start(out=convb_row_b[:], in_=c_convb_row.ap()), writes=[convb_row_b.k], q='pool')
    P.dve(lambda e: e.tensor_copy(out=ident_b[:], in_=ident_f[:]), reads=[ident_f.k], writes=[ident_b.k])
    P.dve(lambda e: e.tensor_copy(out=rot_b[:], in_=mats_f[:, 384:512]), reads=[mats_f.k], writes=[rot_b.k])
    P.dve(lambda e: e.tensor_copy(out=maskb_b[:], in_=mats_f[:, 512:768]), reads=[mats_f.k], writes=[maskb_b.k])
    P.dve(lambda e: e.memset(ones_row_b[:], 1.0), writes=[ones_row_b.k])
    P.act(lambda e: e.activation(out=A_bc[:], in_=bc[:, 32:64], func=AF.Exp), reads=[bc.k], writes=[A_bc.k])
    P.act(lambda e: e.mul(out=A_bc[:], in_=A_bc[:], mul=-1.0), reads=[A_bc.k], writes=[A_bc.k])

    for key in (cfg.get('panel_order') or []):
        get_panel(*key)
    conv_ops = [op for op in P.ops['pool'] if op.dma]
    bar = P.add('sp', lambda e: e.nop())
    for op in conv_ops:
        bar.waits.append(op)
        P.seen_dma['sp'].add(id(op))

    bufA = sb("bufA", [128, TC * D])
    bufB = sb("bufB", [128, DI])
    xn = sb("xn", [128, D], BF16)
    hT = sb("hT", [128, 8, T], BF16)
    h2T = hT
    zs = sb("zs", [128, TC, DI], BF16)
    arena = sb("arena", [128, 24 * (T + 3)], BF16)

    class View:
        def __init__(self, ap, k):
            self.ap = ap
            self.k = k

        def __getitem__(self, key):
            return self.ap[key]
    xbcT = View(arena[:, :].rearrange("p (c t) -> p c t", c=24), arena.k)

    def xview(bf):
        return View(bf[:, :].rearrange("p (c d) -> p c d", c=TC), bf.k)

    def yview(bf):
        return View(bf[:, :], bf.k)
    xcur, ycur = bufA, bufB
    xbuf = xview(xcur)
    ysum = yview(ycur)
    actT = View(arena[:, 0:22 * T].rearrange("p (c t) -> p c t", c=22), arena.k)
    halo = sb("halo", [128, 24, 3], BF16)
    qraw_r = Ring([sb(f"qraw{i}", [128, T], BF16) for i in range(2)])
    ropeT = sb("ropeT", [128, 4, T])
    rtmp1_r = Ring([sb(f"rtmp1_{i}", [128, T]) for i in range(2)])
    rtmp2_r = Ring([sb(f"rtmp2_{i}", [128, T]) for i in range(2)])
    mt2 = rtmp2_r.bufs[0]
    qT = sb("qT", [128, 4, T], BF16)
    kT1 = sb("kT1", [128, 4, 128 + T], BF16)
    Vb1 = sb("Vb1", [128, TC + 1, 512], BF16)
    Pm4 = sb("Pm4", [128, 4, 256], BF16)
    PT4 = sb("PT4", [128, 4, 2, 128], BF16)
    austage = Ring([sb(f"austage{i}", [128, AUW]) for i in range(2)])
    au2 = sb("au2", [128, AUW])
    au3 = sb("au3", [128, AUW])
    xs_tok = sb("xs_tok", [128, DI], BF16)
    xdt = sb("xdt", [128, DI], BF16)
    ynb = xdt
    gT = View(zs[:, :, :].rearrange("p c (k t) -> p (c k) t", t=T), zs.k)
    mixedT = View(xdt[:, :].rearrange("p (k t) -> p k t", k=8), xdt.k)
    Btok = sb("Btok", [128, 512], BF16)
    BT = sb("BT", [128, 4, T], BF16)
    CT = sb("CT", [128, 4, T], BF16)
    Rg_r = Ring([sb(f"Rg{i}", [128, 8, 128]) for i in range(2)])
    Eg_r = Ring([sb(f"Eg{i}", [128, 8, 128], BF16) for i in range(2)])
    CBm_r = Ring([sb(f"CBm{i}", [128, 128], BF16) for i in range(2)])
    S32 = sb("S32", [128, DI])
    Sbf = sb("Sbf", [128, DI], BF16)
    ytmp_r = Ring([sb(f"ytmp{i}", [128, 512]) for i in range(2)])
    num, num2 = ytmp_r.bufs[0], ytmp_r.bufs[1]
    ynT = sb("ynT", [128, 16, T], BF16)
    att = sb("att", [128, 512], BF16)
    attT = sb("attT", [128, 4, T], BF16)
    sg = sb("sg", [128, T], BF16)
    wring = Ring([sb(f"wp{i}", [128, W_EL], BF16) for i in range(5)])
    small = Ring([sb(f"sm{i}", [128, 64]) for i in range(12)])
    psf = Ring([Buf(P, nc.alloc_psum_tensor(f"psf{i}", [128, 512], F32), f"psf{i}") for i in range(6)])
    psb = Ring([Buf(P, nc.alloc_psum_tensor(f"psb{i}", [128, 1024], BF16), f"psb{i}") for i in range(2)])
    for pb in psf.bufs + psb.bufs:
        pb.k.excl = True

    def rms_to_T(xb, dstT, gain_off, c):
        xsrc = xb[:, c, :]
        sm = small.next()
        P.act(lambda e: e.activation(out=xn[:], in_=xsrc, func=AF.Square, accum_out=sm[:, 0:1]),
              reads=[xb.k], writes=[xn.k, sm.k])
        P.act(lambda e: e.activation(out=sm[:, 1:2], in_=sm[:, 0:1], func=AF.Ln, scale=1.0 / D, bias=EPS),
              reads=[sm.k], writes=[sm.k])
        P.act(lambda e: e.activation(out=sm[:, 2:3], in_=sm[:, 1:2], func=AF.Exp, scale=-0.5),
              reads=[sm.k], writes=[sm.k])
        P.dve(lambda e: e.tensor_scalar(out=xn[:], in0=xsrc, scalar1=sm[:, 2:3], scalar2=None, op0=ALU.mult),
              reads=[xb.k, sm.k], writes=[xn.k])
        ps = psb.next()
        for kc in range(8):
            P.pe(lambda e, kc=kc: e.transpose(out=ps[:, kc * 128:(kc + 1) * 128], in_=xn[:, kc * 128:(kc + 1) * 128],
                                              identity=ident_b[:]),
                 reads=[xn.k, ident_b.k], writes=[ps.k])
        P.dve(lambda e: e.tensor_tensor(out=dstT[:, :, c * 128:(c + 1) * 128],
                                        in0=ps[:, :].rearrange("p (k t) -> p k t", k=8),
                                        in1=pp[:, gain_off:gain_off + 8].unsqueeze(2).to_broadcast([128, 8, 128]),
                                        op=ALU.mult),
              reads=[ps.k, pp.k], writes=[dstT.k])
        return sm

    def load_panel(wb, K, f0, ncols, r0=0):
        wbp, tk = get_panel(wb, K, f0, ncols, r0)
        w = wring.next()
        view = w[:, 0:K * ncols].rearrange("p (k f) -> p k f", k=K)
        P.dma(lambda e: e.dma_start(out=w[:, 0:K * ncols], in_=wbp.ap()), reads=[tk], writes=[w.k])
        return w, view

    def proj_fm(wb, K, f0, nfc, rhs, evac):
        per = max(1, min(4, W_EL // (K * 128)))
        fc = 0
        while fc < nfc:
            n = min(per, nfc - fc)
            w, view = load_panel(wb, K, f0 + fc * 128, n * 128)
            for j in range(n):
                ps = psf.next()
                for kc in range(K):
                    P.pe(lambda e, kc=kc, j=j, ps=ps, view=view: e.matmul(
                        ps[:, 0:T], lhsT=view[:, kc, j * 128:(j + 1) * 128], rhs=rhs[:, kc, 0:T],
                        start=(kc == 0), stop=(kc == K - 1)),
                        reads=[w.k, rhs.k], writes=[ps.k])
                evac(fc + j, ps)
            fc += n

    def proj_tm(wb, K, f0, ncols, lhs, evac, panel_cols=None, ksplit=1):
        Kp = K // ksplit
        pc = panel_cols or min(512, (W_EL // Kp) // 128 * 128)
        col0 = 0
        while col0 < ncols:
            wd = min(pc, ncols - col0)
            pss = [psf.next() for _ in range(TC)]
            for kp in range(ksplit):
                w, view = load_panel(wb, Kp, f0 + col0, wd, r0=kp * Kp * 128)
                for c in range(TC):
                    ps = pss[c]
                    for kc in range(Kp):
                        kk = kp * Kp + kc
                        P.pe(lambda e, kc=kc, kk=kk, c=c, ps=ps, view=view, wd=wd: e.matmul(
                            ps[:, 0:wd], lhsT=lhs[:, kk, c * 128:(c + 1) * 128], rhs=view[:, kc, 0:wd],
                            start=(kk == 0), stop=(kk == K - 1)),
                            reads=[w.k, lhs.k], writes=[ps.k])
            for c in range(TC):
                evac(c, col0, wd, pss[c])
            col0 += wd

    def qkv_and_attention(g, b, hsrc, mt_index, blocks, au_rows, mid=None):
        rt = ropeT
        SK = cfg.get('skip', 0)
        if SK & 1:
            P.dve(lambda e: e.memset(rt[:], 1.0), writes=[rt.k])
        else:
            P.dma(lambda e: e.dma_start(out=rt[:], in_=rope_d.ap()[g, :, :, mt_index * T:(mt_index + 1) * T]
                                        .rearrange("f p t -> p f t")), writes=[rt.k])
        kTg = kT1
        Vg = Vb1

        def evac_qk(fc, ps):
            isk = fc >= 4
            h = fc % 4
            qraw = qraw_r.next()
            rtmp1 = rtmp1_r.next()
            rtmp2 = rtmp2_r.next()
            P.act(lambda e: e.activation(out=qraw[:], in_=ps[:, 0:T], func=AF.Copy), reads=[ps.k], writes=[qraw.k])
            if SK & 2:
                return
            ps2 = psf.next()
            P.pe(lambda e: e.matmul(ps2[:, 0:T], lhsT=rot_b[:], rhs=qraw[:], start=True, stop=True),
                 reads=[rot_b.k, qraw.k], writes=[ps2.k])
            if SK & 16:
                return
            P.dve(lambda e: e.tensor_tensor(out=rtmp1[:], in0=ps[:, 0:T], in1=rt[:, 2 * isk, :], op=ALU.mult),
                  reads=[ps.k, rt.k], writes=[rtmp1.k])
            P.dve(lambda e: e.tensor_tensor(out=rtmp2[:], in0=ps2[:, 0:T], in1=rt[:, 2 * isk + 1, :], op=ALU.mult),
                  reads=[ps2.k, rt.k], writes=[rtmp2.k])
            if SK & 32:
                return
            dst = kTg[:, h, 128:128 + T] if isk else qT[:, h, :]
            dk = kTg.k if isk else qT.k
            P.dve(lambda e: e.tensor_tensor(out=dst, in0=rtmp1[:], in1=rtmp2[:], op=ALU.add),
                  reads=[rtmp1.k, rtmp2.k], writes=[dk])

        proj_fm("win", 8, OFF_QKV + g * 512, 4, hsrc, lambda fc, ps: evac_qk(fc, ps))
        proj_fm("win", 8, OFF_QKV + 1536 + g * 512, 4, hsrc, lambda fc, ps: evac_qk(fc + 4, ps))

        def evac_v(c, col0, wd, ps):
            P.act(lambda e: e.activation(out=Vg[:, c + 1, :], in_=ps[:, 0:512], func=AF.Copy),
                  reads=[ps.k], writes=[Vg.k])
        if not SK & 4:
            proj_tm("win", 8, OFF_QKV + 3072 + g * 512, 512, hsrc, evac_v)

        if mid is not None:
            mid()
        AST = cfg.get('astage', 99)
        stages = []
        for j in range(TC if AST >= 3 else 0):
            hp = blocks[j]
            st = austage.next()
            k0 = 0 if hp else 128
            nk = 256 - k0
            nkc = nk // 128
            psS = [psf.next(), psf.next()]
            for h in range(4):
                bnk = psS[h // 2]
                o = (h % 2) * 256
                P.pe(lambda e, h=h, bnk=bnk, o=o: e.matmul(
                    bnk[:, o:o + nk], lhsT=qT[:, h, j * 128:(j + 1) * 128],
                    rhs=kTg[:, h, j * 128 + k0:j * 128 + 256], start=True, stop=False),
                    reads=[qT.k, kTg.k], writes=[bnk.k])
                P.pe(lambda e, bnk=bnk, o=o: e.matmul(
                    bnk[:, o:o + nk], lhsT=ident_b[:], rhs=maskb_b[:, k0:256], start=False, stop=True),
                    reads=[ident_b.k, maskb_b.k], writes=[bnk.k])
            if AST < 4:
                continue
            for pr in range(2):
                bnk = psS[pr]
                P.dve(lambda e, bnk=bnk, pr=pr: e.tensor_reduce(
                    out=st[:, 516 + 2 * pr:518 + 2 * pr], in_=bnk[:, :].rearrange("p (h k) -> p h k", h=2)[:, :, 0:nk],
                    axis=AX.X, op=ALU.max, negate=True),
                    reads=[bnk.k], writes=[st.k])
            for h in range(4):
                bnk = psS[h // 2]
                o = (h % 2) * 256
                P.act(lambda e, bnk=bnk, o=o, h=h: e.activation(
                    out=Pm4[:, h, 0:nk], in_=bnk[:, o:o + nk], func=AF.Exp, bias=st[:, 516 + h:517 + h],
                    accum_out=st[:, 512 + h:513 + h]),
                    reads=[bnk.k, st.k], writes=[Pm4.k, st.k])
            pT = psb.next()
            for h in range(4):
                for kc in range(nkc):
                    P.pe(lambda e, kc=kc, h=h: e.transpose(out=pT[:, (h * 2 + kc) * 128:(h * 2 + kc + 1) * 128],
                                                           in_=Pm4[:, h, kc * 128:(kc + 1) * 128], identity=ident_b[:]),
                         reads=[Pm4.k, ident_b.k], writes=[pT.k])
            P.dve(lambda e: e.tensor_copy(out=PT4[:, :, 0:nkc, :],
                                          in_=pT[:, :].rearrange("p (h c t) -> p h c t", h=4, c=2)[:, :, 0:nkc, :]),
                  reads=[pT.k], writes=[PT4.k])
            psU = psf.next()
            for h in range(4):
                for kc in range(nkc):
                    slot = j + kc if hp else j + 1
                    P.pe(lambda e, kc=kc, h=h, slot=slot: e.matmul(
                        psU[:, h * 128:(h + 1) * 128], lhsT=PT4[:, h, kc, :], rhs=Vg[:, slot, h * 128:(h + 1) * 128],
                        start=(kc == 0), stop=(kc == nkc - 1)),
                        reads=[PT4.k, Vg.k], writes=[psU.k])
            P.act(lambda e, psU=psU, st=st: e.activation(out=st[:, 0:512], in_=psU[:, 0:512], func=AF.Copy),
                  reads=[psU.k], writes=[st.k])
            dst = au_rows(j)
            if dst is not None:
                P.dma(lambda e, dst=dst, st=st: e.dma_start(out=dst, in_=st[:]), reads=[st.k], q='pool')
            stages.append(st)
        if SK & 8:
            return stages
        P.dve(lambda e: e.tensor_copy(out=kTg[:, :, 0:128], in_=kTg[:, :, T:T + 128]), reads=[kTg.k], writes=[kTg.k])
        P.dve(lambda e: e.tensor_copy(out=Vg[:, 0, :], in_=Vg[:, TC, :]), reads=[Vg.k], writes=[Vg.k])
        return stages

    out_ops = []
    for b in range(cfg.get('nb', NB)):
        perm_tiles = []
        for g in cfg.get('perm_groups', (2, 1)):
            L = S // PATTERNS[g][1]
            for mt in range(cfg.get('nmt_perm', NMT)):
                blks = [((mt * T + c * 128) // L, ((mt * T + c * 128) % L) // 128) for c in range(TC)]
                perm_tiles.append((g, mt, blks))

        def load_perm(desc):
            g, mt, blks = desc
            xv = x_d.ap()[b].rearrange("(j r) f -> r j f", r=PATTERNS[g][1])
            for c, (rho, n) in enumerate(blks):
                P.dma(lambda e: e.dma_start(out=xbuf[:, c, :], in_=xv[rho, n * 128:(n + 1) * 128, :]),
                      writes=[xbuf.k], q='pool')

        def load_main(mt, xb):
            for c in range(TC):
                P.dma(lambda e: e.dma_start(out=xb[:, c, :], in_=x_d.ap()[b, mt * T + c * 128:mt * T + (c + 1) * 128, :]),
                      writes=[xb.k], q='pool')

        main_prefetched = False
        if perm_tiles:
            load_perm(perm_tiles[0])
            for c in range(TC):
                rms_to_T(xbuf, hT, PP_NMIX, c)
            if len(perm_tiles) > 1:
                load_perm(perm_tiles[1])
        for i, (g, mt, blks) in enumerate(perm_tiles):
            def mid(i=i):
                if i + 1 < len(perm_tiles):
                    for c in range(TC):
                        rms_to_T(xbuf, hT, PP_NMIX, c)
                    if i + 2 < len(perm_tiles):
                        load_perm(perm_tiles[i + 2])
                    elif cfg.get('nmt', NMT) > 0:
                        load_main(0, xbuf)
            auv = au_d.ap()[b, g - 1].rearrange("(j r) f -> r j f", r=PATTERNS[g][1])
            qkv_and_attention(g, b, hT, mt, [n > 0 for (_, n) in blks],
                              lambda j: auv[blks[j][0], blks[j][1] * 128:(blks[j][1] + 1) * 128, :], mid=mid)
        if perm_tiles and cfg.get('nmt', NMT) > 0:
            if len(perm_tiles) == 1:
                load_main(0, xbuf)
            main_prefetched = True

        rms_done = False
        P.pool(lambda e: e.memset(S32[:], 0.0), writes=[S32.k])
        P.pool(lambda e: e.memset(Sbf[:], 0.0), writes=[Sbf.k])
        P.pool(lambda e: e.memset(halo[:], 0.0), writes=[halo.k])
        for mt in range(cfg.get('nmt', NMT)):
            t0 = mt * T
            if not main_prefetched:
                load_main(mt, xbuf)
            main_prefetched = False
            if not rms_done:
                for c in range(TC):
                    rms_to_T(xbuf, hT, PP_NMIX, c)
            rms_done = False
            if STAGE < 1:
                continue
            def evac_z(c, col0, wd, ps):
                P.act(lambda e: e.activation(out=zs[:, c, col0:col0 + wd], in_=ps[:, 0:wd], func=AF.Silu),
                      reads=[ps.k], writes=[zs.k])
            proj_tm("win", 8, OFF_Z, DI, hT, evac_z)

            P.pool(lambda e: e.tensor_copy(out=xbcT[:, :, 0:3], in_=halo[:]), reads=[halo.k], writes=[xbcT.k])

            def evac_xbc(fc, ps):
                P.act(lambda e: e.activation(out=xbcT[:, fc, 3:3 + T], in_=ps[:, 0:T], func=AF.Copy),
                      reads=[ps.k], writes=[xbcT.k])
            proj_fm("win", 8, OFF_XBC, 24, hT, evac_xbc)
            P.pool(lambda e: e.tensor_copy(out=halo[:], in_=xbcT[:, :, T:T + 3]), reads=[xbcT.k], writes=[halo.k])

            dts = []

            def evac_dt(c, col0, wd, ps):
                sm = small.next()
                P.dve(lambda e: e.tensor_tensor(out=sm[:, 0:32], in0=ps[:, 0:32], in1=bc[:, 0:32], op=ALU.add),
                      reads=[ps.k, bc.k], writes=[sm.k])
                P.act(lambda e: e.activation(out=sm[:, 0:32], in_=sm[:, 0:32], func=AF.Exp), reads=[sm.k], writes=[sm.k])
                P.act(lambda e: e.activation(out=sm[:, 0:32], in_=sm[:, 0:32], func=AF.Ln, bias=1.0),
                      reads=[sm.k], writes=[sm.k])
                P.dve(lambda e: e.tensor_tensor(out=sm[:, 32:64], in0=sm[:, 0:32], in1=A_bc[:], op=ALU.mult),
                      reads=[sm.k, A_bc.k], writes=[sm.k])
                dts.append(sm)
            proj_tm("win", 8, OFF_DT, 32, hT, evac_dt, panel_cols=32)

            if STAGE < 2:
                continue
            st1 = qkv_and_attention(0, b, hT, mt, [(mt * TC + c) > 0 for c in range(TC)], lambda j: None)

            if STAGE < 3:
                continue
            for i in range(8):
                cc = 16 + i
                ps = psf.next()
                for j in range(4):
                    P.pe(lambda e, j=j, cc=cc, ps=ps: e.matmul(ps[:, 0:T], lhsT=wdiag_b[:, j, cc, :],
                                                               rhs=xbcT[:, cc, j:j + T], start=(j == 0), stop=(j == 3)),
                         reads=[wdiag_b.k, xbcT.k], writes=[ps.k])
                dstb = BT if i < 4 else CT
                P.act(lambda e, ps=ps, cc=cc, dstb=dstb, i=i: e.activation(
                    out=dstb[:, i % 4, :], in_=ps[:, 0:T], func=AF.Silu, bias=pp[:, PP_CONVB + cc:PP_CONVB + cc + 1]),
                    reads=[ps.k, pp.k], writes=[dstb.k])

            def conv_tm(c):
                c0 = c * 128
                for blk in range(5):
                    ps = psf.next()
                    for q4 in range(4):
                        cc = blk * 4 + q4
                        for j in range(4):
                            P.pe(lambda e, j=j, cc=cc, ps=ps, q4=q4: e.matmul(
                                ps[:, q4 * 128:(q4 + 1) * 128], lhsT=xbcT[:, cc, c0 + j:c0 + j + 128],
                                rhs=wdiag_b[:, j, cc, :], start=(j == 0), stop=False),
                                reads=[wdiag_b.k, xbcT.k], writes=[ps.k])
                        P.pe(lambda e, cc=cc, ps=ps, q4=q4: e.matmul(
                            ps[:, q4 * 128:(q4 + 1) * 128], lhsT=ones_row_b[0:1, :],
                            rhs=convb_row_b[0:1, cc * 128:(cc + 1) * 128], start=False, stop=True),
                            reads=[ones_row_b.k, convb_row_b.k], writes=[ps.k])
                    if blk < 4:
                        P.act(lambda e, ps=ps, blk=blk: e.activation(out=xs_tok[:, blk * 512:(blk + 1) * 512],
                                                                     in_=ps[:, 0:512], func=AF.Silu),
                              reads=[ps.k], writes=[xs_tok.k])
                    else:
                        P.act(lambda e, ps=ps: e.activation(out=Btok[:], in_=ps[:, 0:512], func=AF.Silu),
                              reads=[ps.k], writes=[Btok.k])


            def scalars(c):
                sm = dts[c]
                c0 = c * 128
                sm2 = small.next()
                sm3 = small.next()
                psA = psf.next()
                P.pe(lambda e, psA=psA, sm=sm: e.matmul(psA[:, 0:32], lhsT=tri_f, rhs=sm[:, 32:64], start=True, stop=True),
                     reads=[mats_f.k, sm.k], writes=[psA.k])
                P.pe(lambda e, psA=psA, sm=sm: e.matmul(psA[:, 32:64], lhsT=ones_f, rhs=sm[:, 32:64], start=True, stop=True),
                     reads=[mats_f.k, sm.k], writes=[psA.k])
                P.dve(lambda e, psA=psA, sm2=sm2: e.tensor_copy(out=sm2[:, 0:32], in_=psA[:, 0:32]),
                      reads=[psA.k], writes=[sm2.k])
                P.act(lambda e, psA=psA, sm2=sm2: e.activation(out=sm2[:, 32:64], in_=psA[:, 0:32], func=AF.Exp),
                      reads=[psA.k], writes=[sm2.k])
                P.dve(lambda e, psA=psA, sm2=sm2, sm3=sm3: e.tensor_tensor(out=sm3[:, 0:32], in0=psA[:, 32:64],
                                                                          in1=sm2[:, 0:32], op=ALU.subtract),
                      reads=[psA.k, sm2.k], writes=[sm3.k])
                P.act(lambda e, sm3=sm3: e.activation(out=sm3[:, 0:32], in_=sm3[:, 0:32], func=AF.Exp),
                      reads=[sm3.k], writes=[sm3.k])
                P.act(lambda e, psA=psA, sm3=sm3: e.activation(out=sm3[:, 32:64], in_=psA[:, 32:64], func=AF.Exp),
                      reads=[psA.k], writes=[sm3.k])
                P.dve(lambda e, sm=sm: e.tensor_tensor(
                    out=xdt[:, :].rearrange("p (h d) -> p h d", h=32), in0=xs_tok[:, :].rearrange("p (h d) -> p h d", h=32),
                    in1=sm[:, 0:32].unsqueeze(2).to_broadcast([128, 32, 64]), op=ALU.mult),
                    reads=[xs_tok.k, sm.k], writes=[xdt.k])
                P.pool(lambda e: e.tensor_tensor(
                    out=ysum[:, :].rearrange("p (h d) -> p h d", h=32), in0=xs_tok[:, :].rearrange("p (h d) -> p h d", h=32),
                    in1=bc[:, 64:96].unsqueeze(2).to_broadcast([128, 32, 64]), op=ALU.mult),
                    reads=[xs_tok.k, bc.k], writes=[ysum.k])
                P.dve(lambda e, sm3=sm3: e.tensor_tensor(
                    out=xs_tok[:, :].rearrange("p (h d) -> p h d", h=32), in0=xdt[:, :].rearrange("p (h d) -> p h d", h=32),
                    in1=sm3[:, 0:32].unsqueeze(2).to_broadcast([128, 32, 64]), op=ALU.mult),
                    reads=[xdt.k, sm3.k], writes=[xs_tok.k])

                return sm2, sm3

            def au_load(c):
                tch = t0 + c * 128
                P.dma(lambda e, tch=tch: e.dma_start(out=au2[:], in_=au_d.ap()[b, 0, tch:tch + 128, :]), writes=[au2.k], q='pool')
                P.dma(lambda e, tch=tch: e.dma_start(out=au3[:], in_=au_d.ap()[b, 1, tch:tch + 128, :]), writes=[au3.k], q='pool')

            def group_front(c, gq):
                sm = dts[c]
                c0 = c * 128
                CBm = CBm_r.next()
                Rg = Rg_r.next()
                Eg = Eg_r.next()
                psC = psf.next()
                P.pe(lambda e: e.matmul(psC[:, 0:128], lhsT=BT[:, gq, c0:c0 + 128],
                                        rhs=CT[:, gq, c0:c0 + 128], start=True, stop=True),
                     reads=[BT.k, CT.k], writes=[psC.k])
                P.dve(lambda e: e.tensor_tensor(out=CBm[:], in0=psC[:, 0:128], in1=tri_f, op=ALU.mult),
                      reads=[psC.k, mats_f.k], writes=[CBm.k])
                P.pool(lambda e: e.tensor_tensor(
                    out=Rg[:], in0=sm[:, 32 + gq * 8:40 + gq * 8].unsqueeze(2).to_broadcast([128, 8, 128]),
                    in1=tri_f.unsqueeze(1).to_broadcast([128, 8, 128]), op=ALU.mult),
                    reads=[sm.k, mats_f.k], writes=[Rg.k])
                for hh in range(2):
                    psD = psf.next()
                    P.pe(lambda e, psD=psD, hh=hh: e.matmul(
                        psD[:, 0:512], lhsT=ustrict_f,
                        rhs=Rg[:, hh * 4:(hh + 1) * 4, :].rearrange("p h l -> p (h l)"), start=True, stop=True),
                        reads=[mats_f.k, Rg.k], writes=[psD.k])
                    P.act(lambda e, psD=psD, hh=hh: e.activation(
                        out=Eg[:, hh * 4:(hh + 1) * 4, :].rearrange("p h l -> p (h l)"), in_=psD[:, 0:512], func=AF.Exp),
                        reads=[psD.k], writes=[Eg.k])
                P.dve(lambda e: e.tensor_tensor(out=Eg[:], in0=Eg[:], in1=CBm[:, :].unsqueeze(1).to_broadcast([128, 8, 128]),
                                                op=ALU.mult),
                      reads=[Eg.k, CBm.k], writes=[Eg.k])
                return Eg

            def group_back(c, gq, Eg, sm2, sm3):
                c0 = c * 128
                ytmp = ytmp_r.next()
                psY = psf.next()
                for hl in range(8):
                    h = gq * 8 + hl
                    P.pe(lambda e, hl=hl, h=h: e.matmul(
                        psY[:, hl * 64:(hl + 1) * 64], lhsT=Eg[:, hl, :], rhs=xdt[:, h * 64:(h + 1) * 64],
                        start=True, stop=True),
                        reads=[Eg.k, xdt.k], writes=[psY.k])
                psO = psf.next()
                P.pe(lambda e: e.matmul(psO[:, 0:512], lhsT=CT[:, gq, c0:c0 + 128],
                                        rhs=Sbf[:, gq * 512:(gq + 1) * 512], start=True, stop=True),
                     reads=[CT.k, Sbf.k], writes=[psO.k])
                gs = slice(gq * 512, (gq + 1) * 512)
                P.dve(lambda e: e.tensor_tensor(
                    out=ytmp[:, :].rearrange("p (h d) -> p h d", h=8), in0=psO[:, 0:512].rearrange("p (h d) -> p h d", h=8),
                    in1=sm2[:, 32 + gq * 8:40 + gq * 8].unsqueeze(2).to_broadcast([128, 8, 64]), op=ALU.mult),
                    reads=[psO.k, sm2.k], writes=[ytmp.k])
                P.dve(lambda e: e.tensor_tensor(out=ysum[:, gs], in0=ysum[:, gs], in1=ytmp[:], op=ALU.add),
                      reads=[ysum.k, ytmp.k], writes=[ysum.k])
                P.dve(lambda e: e.tensor_tensor(out=ysum[:, gs], in0=psY[:, 0:512], in1=ysum[:, gs], op=ALU.add),
                      reads=[psY.k, ysum.k], writes=[ysum.k])
                psN = psf.next()
                P.pe(lambda e: e.matmul(psN[:, 0:512], lhsT=Btok[:, gq * 128:(gq + 1) * 128],
                                        rhs=xs_tok[:, gs], start=True, stop=True),
                     reads=[Btok.k, xs_tok.k], writes=[psN.k])
                P.dve(lambda e: e.tensor_tensor(
                    out=S32[:, gs].rearrange("p (h d) -> p h d", h=8), in0=S32[:, gs].rearrange("p (h d) -> p h d", h=8),
                    in1=sm3[:, 32 + gq * 8:40 + gq * 8].unsqueeze(2).to_broadcast([128, 8, 64]), op=ALU.mult),
                    reads=[S32.k, sm3.k], writes=[S32.k])
                P.dve(lambda e: e.tensor_tensor(out=S32[:, gs], in0=psN[:, 0:512], in1=S32[:, gs], op=ALU.add),
                      reads=[psN.k, S32.k], writes=[S32.k])
                P.act(lambda e: e.activation(out=Sbf[:, gs], in_=S32[:, gs], func=AF.Copy), reads=[S32.k], writes=[Sbf.k])

            def groups(c, sm2, sm3):
                fr = group_front(c, 0)
                for gq in range(4):
                    nxt = group_front(c, gq + 1) if gq < 3 else None
                    group_back(c, gq, fr, sm2, sm3)
                    fr = nxt

            def tail(c):
                c0 = c * 128
                P.dve(lambda e, c=c: e.tensor_tensor(out=ysum[:], in0=ysum[:], in1=zs[:, c, :], op=ALU.mult),
                      reads=[ysum.k, zs.k], writes=[ysum.k])
                sm4 = small.next()
                for gq in range(4):
                    P.act(lambda e, gq=gq, sm4=sm4: e.activation(out=ytmp_r.bufs[0][:],
                                                                 in_=ysum[:, gq * 512:(gq + 1) * 512], func=AF.Square,
                                                                 accum_out=sm4[:, gq:gq + 1]),
                          reads=[ysum.k], writes=[ytmp_r.bufs[0].k, sm4.k])
                P.act(lambda e, sm4=sm4: e.activation(out=sm4[:, 4:8], in_=sm4[:, 0:4], func=AF.Ln, scale=1.0 / 512, bias=EPS),
                      reads=[sm4.k], writes=[sm4.k])
                P.act(lambda e, sm4=sm4: e.activation(out=sm4[:, 8:12], in_=sm4[:, 4:8], func=AF.Exp, scale=-0.5),
                      reads=[sm4.k], writes=[sm4.k])
                for gq in range(4):
                    P.act(lambda e, gq=gq, sm4=sm4: e.activation(out=ynb[:, gq * 512:(gq + 1) * 512],
                                                                 in_=ysum[:, gq * 512:(gq + 1) * 512], func=AF.Copy,
                                                                 scale=sm4[:, 8 + gq:9 + gq]),
                          reads=[ysum.k, sm4.k], writes=[ynb.k])
                for half in range(2):
                    pT = psb.next()
                    for kc in range(8):
                        P.pe(lambda e, kc=kc, half=half, pT=pT: e.transpose(
                            out=pT[:, kc * 128:(kc + 1) * 128],
                            in_=ynb[:, (half * 8 + kc) * 128:(half * 8 + kc + 1) * 128], identity=ident_b[:]),
                            reads=[ynb.k, ident_b.k], writes=[pT.k])
                    P.dve(lambda e, half=half, pT=pT: e.tensor_tensor(
                        out=ynT[:, half * 8:(half + 1) * 8, c0:c0 + 128], in0=pT[:, :].rearrange("p (k t) -> p k t", k=8),
                        in1=pp[:, PP_SSMN + half * 8:PP_SSMN + half * 8 + 8].unsqueeze(2).to_broadcast([128, 8, 128]),
                        op=ALU.mult),
                        reads=[pT.k, pp.k], writes=[ynT.k])


            def merge(c):
                c0 = c * 128
                s1 = st1[c]
                sm5 = small.next()
                srcs = [s1, au2, au3]
                P.dve(lambda e, sm5=sm5, s1=s1: e.tensor_tensor(out=sm5[:, 0:4], in0=s1[:, 516:520], in1=au2[:, 516:520], op=ALU.min),
                      reads=[s1.k, au2.k], writes=[sm5.k])
                P.dve(lambda e, sm5=sm5: e.tensor_tensor(out=sm5[:, 0:4], in0=sm5[:, 0:4], in1=au3[:, 516:520], op=ALU.min),
                      reads=[sm5.k, au3.k], writes=[sm5.k])
                for gi, sr in enumerate(srcs):
                    P.dve(lambda e, sm5=sm5, sr=sr, gi=gi: e.tensor_tensor(out=sm5[:, 4 + gi * 4:8 + gi * 4], in0=sr[:, 516:520],
                                                                           in1=sm5[:, 0:4], op=ALU.subtract),
                          reads=[sr.k, sm5.k], writes=[sm5.k])
                P.act(lambda e, sm5=sm5: e.activation(out=sm5[:, 16:28], in_=sm5[:, 4:16], func=AF.Exp, scale=-1.0),
                      reads=[sm5.k], writes=[sm5.k])
                for gi, sr in enumerate(srcs):
                    P.dve(lambda e, sm5=sm5, sr=sr, gi=gi: e.tensor_tensor(out=sm5[:, 28 + gi * 4:32 + gi * 4], in0=sr[:, 512:516],
                                                                           in1=sm5[:, 16 + gi * 4:20 + gi * 4], op=ALU.mult),
                          reads=[sr.k, sm5.k], writes=[sm5.k])
                P.dve(lambda e, sm5=sm5: e.tensor_tensor(out=sm5[:, 40:44], in0=sm5[:, 28:32], in1=sm5[:, 32:36], op=ALU.add),
                      reads=[sm5.k], writes=[sm5.k])
                P.dve(lambda e, sm5=sm5: e.tensor_tensor(out=sm5[:, 40:44], in0=sm5[:, 40:44], in1=sm5[:, 36:40], op=ALU.add),
                      reads=[sm5.k], writes=[sm5.k])
                P.dve(lambda e, sm5=sm5: e.reciprocal(out=sm5[:, 44:48], in_=sm5[:, 40:44]), reads=[sm5.k], writes=[sm5.k])

                def fb(gi, sm5=sm5):
                    return sm5[:, 16 + gi * 4:20 + gi * 4].unsqueeze(2).to_broadcast([128, 4, 128])

                def v4(bf, lo=0):
                    return bf[:, lo:lo + 512].rearrange("p (h d) -> p h d", h=4)
                P.dve(lambda e, s1=s1: e.tensor_tensor(out=v4(num), in0=v4(s1), in1=fb(0), op=ALU.mult),
                      reads=[s1.k, sm5.k], writes=[num.k])
                P.pool(lambda e: e.tensor_tensor(out=v4(num2), in0=v4(au2), in1=fb(1), op=ALU.mult),
                       reads=[au2.k, sm5.k], writes=[num2.k])
                P.pool(lambda e: e.tensor_tensor(out=num[:], in0=num[:], in1=num2[:], op=ALU.add),
                       reads=[num.k, num2.k], writes=[num.k])
                P.dve(lambda e: e.tensor_tensor(out=v4(num2), in0=v4(au3), in1=fb(2), op=ALU.mult),
                      reads=[au3.k, sm5.k], writes=[num2.k])
                P.pool(lambda e: e.tensor_tensor(out=num[:], in0=num[:], in1=num2[:], op=ALU.add),
                       reads=[num.k, num2.k], writes=[num.k])
                P.dve(lambda e, sm5=sm5: e.tensor_tensor(out=v4(att), in0=v4(num),
                                                         in1=sm5[:, 44:48].unsqueeze(2).to_broadcast([128, 4, 128]), op=ALU.mult),
                      reads=[num.k, sm5.k], writes=[att.k])
                pT = psb.next()
                for kc in range(4):
                    P.pe(lambda e, kc=kc, pT=pT: e.transpose(out=pT[:, kc * 128:(kc + 1) * 128],
                                                             in_=att[:, kc * 128:(kc + 1) * 128], identity=ident_b[:]),
                         reads=[att.k, ident_b.k], writes=[pT.k])
                P.act(lambda e, pT=pT: e.activation(out=attT[:, :, c0:c0 + 128],
                                                    in_=pT[:, 0:512].rearrange("p (k t) -> p k t", k=4), func=AF.Copy),
                      reads=[pT.k], writes=[attT.k])


            conv_tm(0)
            ss = scalars(0)
            au_load(0)
            groups(0, *ss)
            for c in range(1, TC):
                conv_tm(c)
                tail(c - 1)
                merge(c - 1)
                ss = scalars(c)
                au_load(c)
                groups(c, *ss)
            tail(TC - 1)
            merge(TC - 1)

            def evac_g(fc, ps):
                P.act(lambda e: e.activation(out=gT[:, fc, :], in_=ps[:, 0:T], func=AF.Sigmoid,
                                             bias=pp[:, PP_BG + fc:PP_BG + fc + 1]),
                      reads=[ps.k, pp.k], writes=[gT.k])
            proj_fm("win", 8, OFF_G, 16, hT, evac_g)


            if STAGE < 5:
                continue
            pend = {}

            def evac_ssm(fc, ps):
                P.dve(lambda e: e.tensor_tensor(out=ysum[:, fc * T:(fc + 1) * T], in0=ps[:, 0:T], in1=gT[:, fc, :], op=ALU.mult),
                      reads=[ps.k, gT.k], writes=[ysum.k])
            proj_fm("wso", 16, 0, 8, ynT, evac_ssm)

            def evac_att(fc, ps):
                P.dve(lambda e: e.tensor_tensor(out=mt2[:], in0=ps[:, 0:T], in1=gT[:, 8 + fc, :], op=ALU.mult),
                      reads=[ps.k, gT.k], writes=[mt2.k])
                P.pool(lambda e: e.tensor_tensor(out=mixedT[:, fc, :], in0=ysum[:, fc * T:(fc + 1) * T], in1=mt2[:], op=ALU.add),
                       reads=[mt2.k, ysum.k], writes=[mixedT.k])
            proj_fm("wao", 4, 0, 8, attT, evac_att)
            if mt + 1 < cfg.get('nmt', NMT):
                load_main(mt + 1, xview(ycur))
                main_prefetched = True

            def evac_mix(c, col0, wd, ps):
                P.dve(lambda e: e.tensor_tensor(out=xbuf[:, c, col0:col0 + wd], in0=ps[:, 0:wd], in1=xbuf[:, c, col0:col0 + wd],
                                                op=ALU.add),
                      reads=[ps.k, xbuf.k], writes=[xbuf.k])
            proj_tm("wmx", 8, 0, D, mixedT, evac_mix)

            if STAGE < 6:
                continue
            for c in range(TC):
                rms_to_T(xbuf, h2T, PP_NFFN, c)
            for fb0 in range(0, 22, 4):
                nfc = min(4, 22 - fb0)
                wg_, vg_ = load_panel("wg", 8, fb0 * 128, nfc * 128)
                wu_, vu_ = load_panel("wu", 8, fb0 * 128, nfc * 128)
                for j in range(nfc):
                    psg = psf.next()
                    psu = psf.next()
                    for kc in range(8):
                        P.pe(lambda e, kc=kc, j=j, psg=psg, vg_=vg_: e.matmul(psg[:, 0:T], lhsT=vg_[:, kc, j * 128:(j + 1) * 128],
                                                                             rhs=h2T[:, kc, :], start=(kc == 0), stop=(kc == 7)),
                             reads=[wg_.k, h2T.k], writes=[psg.k])
                    for kc in range(8):
                        P.pe(lambda e, kc=kc, j=j, psu=psu, vu_=vu_: e.matmul(psu[:, 0:T], lhsT=vu_[:, kc, j * 128:(j + 1) * 128],
                                                                             rhs=h2T[:, kc, :], start=(kc == 0), stop=(kc == 7)),
                             reads=[wu_.k, h2T.k], writes=[psu.k])
                    P.act(lambda e, psg=psg: e.activation(out=sg[:], in_=psg[:, 0:T], func=AF.Silu), reads=[psg.k], writes=[sg.k])
                    P.dve(lambda e, psu=psu, j=j, fb0=fb0: e.tensor_tensor(out=actT[:, fb0 + j, :], in0=psu[:, 0:T], in1=sg[:], op=ALU.mult),
                          reads=[psu.k, sg.k], writes=[actT.k])

            if main_prefetched:
                for c in range(TC):
                    rms_to_T(xview(ycur), hT, PP_NMIX, c)
                rms_done = True

            def evac_down(c, col0, wd, ps):
                P.dve(lambda e: e.tensor_tensor(out=xbuf[:, c, col0:col0 + wd], in0=ps[:, 0:wd], in1=xbuf[:, c, col0:col0 + wd],
                                                op=ALU.add),
                      reads=[ps.k, xbuf.k], writes=[xbuf.k])
            proj_tm("wd", 22, 0, D, actT, evac_down, panel_cols=256, ksplit=2)

            for c in range(TC):
                sm = small.next()
                P.act(lambda e, c=c, sm=sm: e.activation(out=xn[:], in_=xbuf[:, c, :], func=AF.Square, accum_out=sm[:, 0:1]),
                      reads=[xbuf.k], writes=[xn.k, sm.k])
                P.act(lambda e, sm=sm: e.activation(out=sm[:, 1:2], in_=sm[:, 0:1], func=AF.Ln, scale=1.0 / D, bias=EPS),
                      reads=[sm.k], writes=[sm.k])
                P.act(lambda e, sm=sm: e.activation(out=sm[:, 2:3], in_=sm[:, 1:2], func=AF.Exp, scale=-0.5),
                      reads=[sm.k], writes=[sm.k])
                P.dve(lambda e, c=c, sm=sm: e.scalar_tensor_tensor(out=xbuf[:, c, :], in0=xbuf[:, c, :], scalar=sm[:, 2:3], in1=nf[:],
                                                                   op0=ALU.mult, op1=ALU.mult),
                      reads=[xbuf.k, sm.k, nf.k], writes=[xbuf.k])
                o = P.dma(lambda e, c=c: e.dma_start(out=out_d.ap()[b, t0 + c * 128:t0 + (c + 1) * 128, :], in_=xbuf[:, c, :]),
                          reads=[xbuf.k], q='pool')
                out_ops.append(o)
            if main_prefetched:
                xcur, ycur = ycur, xcur
                xbuf = xview(xcur)
                ysum = yview(ycur)

    finals = [op for op in P.dma_last['pool'].values()]
    P.emit(final_waits=finals)
    nc._panel_order = list(panel_order)
    return nc


def _consts(inputs):
    f = np.float32
    i = np.arange(128)
    tri = (i[:, None] <= i[None, :]).astype(f)
    ustrict = (i[:, None] > i[None, :]).astype(f)
    ones = np.ones((128, 128), f)
    rot = np.zeros((128, 128), f)
    for dp in range(64):
        rot[dp + 64, dp] = -1.0
        rot[dp, dp + 64] = 1.0
    maskb = np.full((128, 256), NEG, f)
    q = i[:, None]
    kk = i[None, :]
    maskb[:, 0:128][kk >= q] = 0.0
    maskb[:, 128:256][kk <= q] = 0.0
    mats = np.concatenate([tri, ustrict, ones, rot, maskb], axis=1)
    conv_w = np.asarray(inputs["conv_w"], f)[0]
    wdiag = np.zeros((128, 4, 24, 128), f)
    for j in range(4):
        for cc in range(24):
            wdiag[i, j, cc, i] = conv_w[j, cc * 128:(cc + 1) * 128]
    conv_b = np.asarray(inputs["conv_b"], f)[0]
    pp = np.zeros((128, 72), f)
    pp[:, 0:24] = conv_b.reshape(24, 128).T
    pp[:, 24:40] = np.asarray(inputs["b_gate"], f)[0].reshape(16, 128).T
    pp[:, 40:56] = np.asarray(inputs["ssm_norm"], f)[0].reshape(16, 128).T
    pp[:, 56:64] = np.asarray(inputs["norm_mix"], f)[0].reshape(8, 128).T
    pp[:, 64:72] = np.asarray(inputs["norm_ffn"], f)[0].reshape(8, 128).T
    bcm = np.zeros((128, 96), f)
    bcm[:, 0:32] = np.asarray(inputs["dt_bias"], f)[0][None, :]
    bcm[:, 32:64] = np.asarray(inputs["a_log"], f)[0][None, :]
    bcm[:, 64:96] = np.asarray(inputs["d_skip"], f)[0][None, :]
    nfm = np.broadcast_to(np.asarray(inputs["norm_final"], f)[None, :], (128, D)).copy()
    half = 64
    inv = (10000.0 ** (-np.arange(half, dtype=np.float64) / half))
    pos = np.arange(S, dtype=np.float64)
    ang = pos[None, :] * np.concatenate([inv, inv])[:, None]
    cos, sin = np.cos(ang), np.sin(ang)
    rope = np.zeros((3, 4, 128, S), f)
    sc = 128.0 ** -0.5
    for g, (_, r) in enumerate(PATTERNS):
        L = S // r
        idx = np.arange(S)
        perm = (idx % L) * r + idx // L
        rope[g, 0] = cos[:, perm] * sc
        rope[g, 1] = sin[:, perm] * sc
        rope[g, 2] = cos[:, perm]
        rope[g, 3] = sin[:, perm]
    return dict(c_mats=mats, c_ident=np.eye(128, dtype=f), c_wdiag=wdiag.reshape(128, -1),
                c_convb_row=conv_b[None, :].copy(), c_pp=pp, c_bc=bcm, c_nf=nfm, rope=rope)


_NC_CACHE = {}


def kernel(**inputs):
    f = np.float32
    if "nc" not in _NC_CACHE:
        order = build_nc()._panel_order
        _NC_CACHE["nc"] = build_nc({"panel_order": order})
    nc = _NC_CACHE["nc"]
    x = np.ascontiguousarray(np.asarray(inputs["x"], f))
    shared = _consts(inputs)
    for k_in, k_dev in (("w_in", "w_in"), ("w_ssm_out", "w_ssm_out"), ("w_att_out", "w_att_out"),
                        ("w_mix_out", "w_mix_out"), ("w_ffn_gate", "w_ffn_gate"), ("w_ffn_up", "w_ffn_up"),
                        ("w_ffn_down", "w_ffn_down")):
        shared[k_dev] = np.ascontiguousarray(np.asarray(inputs[k_in], f)[0])
    in_maps = []
    for c in range(8):
        m = dict(shared)
        m["x"] = x[c * NB:(c + 1) * NB]
        in_maps.append(m)
    res = run_bass_kernel_spmd(nc, in_maps, core_ids=list(range(8)))
    return np.concatenate([r["out"] for r in res.results], axis=0).astype(f)
```

```python
import numpy as np
import concourse.bass as bass
import concourse.mybir as mybir
from concourse.bass_utils import run_bass_kernel_spmd

F32 = mybir.dt.float32
BF16 = mybir.dt.bfloat16
AF = mybir.ActivationFunctionType
ALU = mybir.AluOpType
AX = mybir.AxisListType

S = 4096
D = 1024
NB = 2
TC = 2
T = TC * 128
NMT = S // T
DI = 2048
CONVD = 3072
DFF = 2816
INP = 11808
OFF_Z, OFF_XBC, OFF_DT, OFF_QKV, OFF_G = 0, 2048, 5120, 5152, 9760
EPS = 1e-6
NEG = -30000.0
PATTERNS = ((128, 1), (512, 4), (2048, 16))
AUW = 520
W_EL = 4096
ENGS = ['pe', 'act', 'dve', 'pool', 'sp']


class Tok:
    __slots__ = ('name', 'w', 'r', 'excl')

    def __init__(self, name):
        self.name = name
        self.w = None
        self.r = []
        self.excl = False


class Op:
    __slots__ = ('eng', 'idx', 'fn', 'waits', 'signal', 'sigval', 'dma', 'sem', 'semval')

    def __init__(self, eng, idx, fn, dma):
        self.eng = eng
        self.idx = idx
        self.fn = fn
        self.waits = []
        self.signal = False
        self.sigval = None
        self.dma = dma
        self.sem = None
        self.semval = None


class Rec:
    def __init__(self):
        self.call = None

    def __getattr__(self, name):
        def f(*a, **k):
            self.call = (name, a, k)
            return self
        return f


class Prog:
    def __init__(self, nc):
        self.nc = nc
        self.ops = {e: [] for e in ENGS}
        self.seen = {e: {} for e in ENGS}
        self.seen_dma = {e: set() for e in ENGS}
        self.cnt = {e: nc.alloc_semaphore(name=f"cnt_{e}") for e in ['pe', 'act', 'dve', 'pool']}
        self.dma_pool = {q: [nc.alloc_semaphore(name=f"dma_{q}_{i}") for i in range(n)]
                         for q, n in (('sp', 12), ('act', 4), ('pool', 16))}
        self.dma_count = {q: 0 for q in self.dma_pool}
        self.dma_last = {q: {} for q in self.dma_pool}
        self.ntok = 0

    def tok(self, name=None):
        self.ntok += 1
        return Tok(name or f"t{self.ntok}")

    def add(self, eng, fn, reads=(), writes=(), dma=False):
        ops = self.ops[eng]
        rec = Rec()
        fn(rec)
        op = Op(eng, len(ops), rec.call, dma)
        deps = {}
        for t in reads:
            if t.w is not None:
                deps[id(t.w)] = (t.w, True)
            if t.excl:
                for r in t.r:
                    if r.eng != eng and id(r) not in deps:
                        deps[id(r)] = (r, False)
        for t in writes:
            if t.w is not None and id(t.w) not in deps:
                deps[id(t.w)] = (t.w, False)
            for r in t.r:
                if id(r) not in deps:
                    deps[id(r)] = (r, False)
        seen = self.seen[eng]
        best = {}
        for d, raw in deps.values():
            if d.dma:
                if id(d) in self.seen_dma[eng]:
                    continue
                self.seen_dma[eng].add(id(d))
                op.waits.append(d)
            else:
                if d.eng == eng and not dma:
                    if eng == 'pe' or not raw:
                        continue
                if seen.get(d.eng, -1) >= d.idx:
                    continue
                if d.eng not in best or best[d.eng].idx < d.idx:
                    best[d.eng] = d
        for d in best.values():
            seen[d.eng] = d.idx
            d.signal = True
            op.waits.append(d)
        if dma:
            q = eng
            k = self.dma_count[q]
            self.dma_count[q] = k + 1
            pool = self.dma_pool[q]
            si = k % len(pool)
            op.sem = pool[si]
            op.semval = 16 * (k // len(pool) + 1)
            prev = self.dma_last[q].get(si)
            if prev is not None and id(prev) not in self.seen_dma[eng]:
                self.seen_dma[eng].add(id(prev))
                op.waits.append(prev)
            self.dma_last[q][si] = op
        ops.append(op)
        for t in reads:
            t.r.append(op)
        for t in writes:
            t.w = op
            t.r = []
        return op

    def pe(self, fn, reads=(), writes=()):
        return self.add('pe', fn, reads, writes)

    def act(self, fn, reads=(), writes=()):
        return self.add('act', fn, reads, writes)

    def dve(self, fn, reads=(), writes=()):
        return self.add('dve', fn, reads, writes)

    def pool(self, fn, reads=(), writes=()):
        return self.add('pool', fn, reads, writes)

    def dma(self, fn, reads=(), writes=(), q='sp'):
        return self.add(q, fn, reads, writes, dma=True)

    def emit(self, final_waits=()):
        nc = self.nc
        for e in ['pe', 'act', 'dve', 'pool']:
            c = 0
            for op in self.ops[e]:
                if op.dma:
                    continue
                if op.signal:
                    c += 1
                    op.sigval = c
        cnt = self.cnt

        def run(engname, engobj, extra=()):
            for op in self.ops[engname]:
                for d in op.waits:
                    if d.dma:
                        engobj.wait_ge(d.sem, d.semval)
                    else:
                        engobj.wait_ge(cnt[d.eng], d.sigval)
                name, a, k = op.fn
                ins = getattr(engobj, name)(*a, **k)
                if op.dma:
                    ins.then_inc(op.sem, 16)
                elif op.signal:
                    ins.then_inc(cnt[op.eng], 1)
            for d in extra:
                engobj.wait_ge(d.sem, d.semval)

        with nc.Block() as block:
            @block.tensor
            def _(e):
                run('pe', e)

            @block.scalar
            def _(e):
                run('act', e)

            @block.vector
            def _(e):
                run('dve', e)

            @block.gpsimd
            def _(e):
                run('pool', e, extra=[d for d in final_waits if d.eng == 'pool'])

            @block.sync
            def _(e):
                run('sp', e, extra=[d for d in final_waits if d.eng == 'sp'])


class Buf:
    def __init__(self, P, t, name):
        self.t = t
        self.k = P.tok(name)

    def __getitem__(self, key):
        return self.t[key]


class Ring:
    def __init__(self, bufs):
        self.bufs = bufs
        self.i = 0

    def next(self):
        b = self.bufs[self.i % len(self.bufs)]
        self.i += 1
        return b


def build_nc(cfg=None):
    cfg = cfg or {}
    STAGE = cfg.get('stage', 99)
    nc = bass.Bass("TRN2", target_bir_lowering=False)
    P = Prog(nc)

    def dram_in(name, shape):
        return nc.dram_tensor(name, list(shape), F32, kind="ExternalInput")

    x_d = dram_in("x", [NB, S, D])
    win_d = dram_in("w_in", [D, INP])
    wso_d = dram_in("w_ssm_out", [DI, D])
    wao_d = dram_in("w_att_out", [512, D])
    wmx_d = dram_in("w_mix_out", [D, D])
    wg_d = dram_in("w_ffn_gate", [D, DFF])
    wu_d = dram_in("w_ffn_up", [D, DFF])
    wd_d = dram_in("w_ffn_down", [DFF, D])
    c_mats = dram_in("c_mats", [128, 4 * 128 + 256])
    c_ident = dram_in("c_ident", [128, 128])
    c_wdiag = dram_in("c_wdiag", [128, 4 * 24 * 128])
    c_convb_row = dram_in("c_convb_row", [1, CONVD])
    c_pp = dram_in("c_pp", [128, 72])
    c_bc = dram_in("c_bc", [128, 96])
    c_nf = dram_in("c_nf", [128, D])
    rope_d = dram_in("rope", [3, 4, 128, S])
    out_d = nc.dram_tensor("out", [NB, S, D], F32, kind="ExternalOutput")
    au_d = nc.dram_tensor("au", [NB, 2, S, AUW], F32, kind="Internal")

    def dram_bf(name, shape):
        return nc.dram_tensor(name, list(shape), BF16, kind="Internal")


    def sb(name, shape, dt=F32):
        return Buf(P, nc.alloc_sbuf_tensor(name, list(shape), dt), name)

    mats_f = sb("mats_f", [128, 4 * 128 + 256])
    ident_f = sb("ident_f", [128, 128])
    ident_b = sb("ident_b", [128, 128], BF16)
    rot_b = sb("rot_b", [128, 128], BF16)
    maskb_b = sb("maskb_b", [128, 256], BF16)
    wdiag_b = sb("wdiag_b", [128, 4, 24, 128], BF16)
    convb_row_b = sb("convb_row_b", [1, CONVD], BF16)
    ones_row_b = sb("ones_row_b", [1, 128], BF16)
    pp = sb("pp", [128, 72])
    bc = sb("bc", [128, 96])
    nf = sb("nf", [128, D])
    A_bc = sb("A_bc", [128, 32])
    tri_f = mats_f[:, 0:128]
    ustrict_f = mats_f[:, 128:256]
    ones_f = mats_f[:, 256:384]
    PP_CONVB, PP_BG, PP_SSMN, PP_NMIX, PP_NFFN = 0, 24, 40, 56, 64

    wsrc = {"win": win_d, "wso": wso_d, "wao": wao_d, "wmx": wmx_d, "wg": wg_d, "wu": wu_d, "wd": wd_d}
    panel_cache = {}
    panel_order = []

    def get_panel(name, K, f0, ncols, r0=0):
        key = (name, K, f0, ncols, r0)
        if key not in panel_cache:
            wbp = nc.dram_tensor(f"wp_{len(panel_cache)}", [128, K * ncols], BF16, kind="Internal")
            tk = P.tok(f"wp_{len(panel_cache)}")
            src = wsrc[name]
            P.dma(lambda e: e.dma_start(out=wbp.ap().rearrange("p (k f) -> p k f", k=K),
                                        in_=src.ap()[r0:r0 + K * 128, f0:f0 + ncols].rearrange("(k p) f -> p k f", p=128)),
                  writes=[tk], q='pool')
            panel_cache[key] = (wbp, tk)
            panel_order.append(key)
        return panel_cache[key]

    P.dma(lambda e: e.dma_start(out=mats_f[:], in_=c_mats.ap()), writes=[mats_f.k])
    P.dma(lambda e: e.dma_start(out=ident_f[:], in_=c_ident.ap()), writes=[ident_f.k])
    P.dma(lambda e: e.dma_start(out=pp[:], in_=c_pp.ap()), writes=[pp.k])
    P.dma(lambda e: e.dma_start(out=bc[:], in_=c_bc.ap()), writes=[bc.k])
    P.dma(lambda e: e.dma_start(out=nf[:], in_=c_nf.ap()), writes=[nf.k])
    if cfg.get('misc', 3) & 1:
        P.dma(lambda e: e.dma_start(out=wdiag_b[:], in_=c_wdiag.ap().rearrange("p (j c f) -> p j c f", j=4, c=24)),
              writes=[wdiag_b.k], q='pool')
    if cfg.get('misc', 3) & 2:
        P.dma(lambda e: e.dma_start(out=convb_row_b[:], in_=c_convb_row.ap()), writes=[convb_row_b.k], q='pool')
    P.dve(lambda e: e.tensor_copy(out=ident_b[:], in_=ident_f[:]), reads=[ident_f.k], writes=[ident_b.k])
    P.dve(lambda e: e.tensor_copy(out=rot_b[:], in_=mats_f[:, 384:512]), reads=[mats_f.k], writes=[rot_b.k])
    P.dve(lambda e: e.tensor_copy(out=maskb_b[:], in_=mats_f[:, 512:768]), reads=[mats_f.k], writes=[maskb_b.k])
    P.dve(lambda e: e.memset(ones_row_b[:], 1.0), writes=[ones_row_b.k])
    P.act(lambda e: e.activation(out=A_bc[:], in_=bc[:, 32:64], func=AF.Exp), reads=[bc.k], writes=[A_bc.k])
    P.act(lambda e: e.mul(out=A_bc[:], in_=A_bc[:], mul=-1.0), reads=[A_bc.k], writes=[A_bc.k])

    for key in (cfg.get('panel_order') or []):
        get_panel(*key)
    conv_ops = [op for op in P.ops['pool'] if op.dma]
    bar = P.add('sp', lambda e: e.nop())
    for op in conv_ops:
        bar.waits.append(op)
        P.seen_dma['sp'].add(id(op))

    bufA = sb("bufA", [128, TC * D])
    bufB = sb("bufB", [128, DI])
    xn = sb("xn", [128, D], BF16)
    hT = sb("hT", [128, 8, T], BF16)
    h2T = hT
    zs = sb("zs", [128, TC, DI], BF16)
    arena = sb("arena", [128, 24 * (T + 3)], BF16)

    class View:
        def __init__(self, ap, k):
            self.ap = ap
            self.k = k

        def __getitem__(self, key):
            return self.ap[key]
    xbcT = View(arena[:, :].rearrange("p (c t) -> p c t", c=24), arena.k)

    def xview(bf):
        return View(bf[:, :].rearrange("p (c d) -> p c d", c=TC), bf.k)

    def yview(bf):
        return View(bf[:, :], bf.k)
    xcur, ycur = bufA, bufB
    xbuf = xview(xcur)
    ysum = yview(ycur)
    actT = View(arena[:, 0:22 * T].rearrange("p (c t) -> p c t", c=22), arena.k)
    halo = sb("halo", [128, 24, 3], BF16)
    qraw_r = Ring([sb(f"qraw{i}", [128, T], BF16) for i in range(1)])
    rtmp1_r = Ring([sb(f"rtmp1_{i}", [128, T]) for i in range(2)])
    rtmp2_r = Ring([sb(f"rtmp2_{i}", [128, T]) for i in range(2)])
    mt2 = rtmp2_r.bufs[0]
    qT = sb("qT", [128, 4, T], BF16)
    kT1 = sb("kT1", [128, 4, 128 + T], BF16)
    Vb1 = sb("Vb1", [128, TC + 1, 512], BF16)
    Pm4 = sb("Pm4", [128, 4, 256], BF16)
    PT4 = sb("PT4", [128, 4, 2, 128], BF16)
    austage = Ring([sb(f"austage{i}", [128, AUW]) for i in range(2)])
    au3 = sb("au3", [128, AUW])
    xs_tok = sb("xs_tok", [128, DI], BF16)
    xdt = sb("xdt", [128, DI], BF16)
    ynb = xdt
    gT = View(zs[:, :, :].rearrange("p c (k t) -> p (c k) t", t=T), zs.k)
    mixedT = View(xdt[:, :].rearrange("p (k t) -> p k t", k=8), xdt.k)
    Btok = sb("Btok", [128, 512], BF16)
    BT = sb("BT", [128, 4, T], BF16)
    CT = sb("CT", [128, 4, T], BF16)
    Rg_r = Ring([sb(f"Rg{i}", [128, 8, 128]) for i in range(2)])
    Eg_r = Ring([sb(f"Eg{i}", [128, 8, 128], BF16) for i in range(2)])
    _rg0, _rg1, _eg0 = Rg_r.bufs[0], Rg_r.bufs[1], Eg_r.bufs[0]
    ropeT = View(_rg0[:, :, :].rearrange("p (f a) l -> p f (a l)", f=4), _rg0.k)
    au2 = View(_rg1[:, :, :].rearrange("p h l -> p (h l)")[:, 0:AUW], _rg1.k)
    att = View(_eg0[:, :, :].rearrange("p h l -> p (h l)")[:, 0:512], _eg0.k)
    CBm_r = Ring([sb(f"CBm{i}", [128, 128], BF16) for i in range(2)])
    S32 = sb("S32", [128, DI])
    Sbf = sb("Sbf", [128, DI], BF16)
    ytmp_r = Ring([sb(f"ytmp{i}", [128, 512]) for i in range(2)])
    num, num2 = ytmp_r.bufs[0], ytmp_r.bufs[1]
    ynT = sb("ynT", [128, 16, T], BF16)
    attT = sb("attT", [128, 4, T], BF16)
    sg = sb("sg", [128, T], BF16)
    wring = Ring([sb(f"wp{i}", [128, W_EL], BF16) for i in range(6)])
    small = Ring([sb(f"sm{i}", [128, 64]) for i in range(12)])
    psf = Ring([Buf(P, nc.alloc_psum_tensor(f"psf{i}", [128, 512], F32), f"psf{i}") for i in range(6)])
    psb = Ring([Buf(P, nc.alloc_psum_tensor(f"psb{i}", [128, 1024], BF16), f"psb{i}") for i in range(2)])
    for pb in psf.bufs + psb.bufs:
        pb.k.excl = True

    def rms_to_T(xb, dstT, gain_off, c):
        xsrc = xb[:, c, :]
        sm = small.next()
        P.act(lambda e: e.activation(out=xn[:], in_=xsrc, func=AF.Square, accum_out=sm[:, 0:1]),
              reads=[xb.k], writes=[xn.k, sm.k])
        P.act(lambda e: e.activation(out=sm[:, 1:2], in_=sm[:, 0:1], func=AF.Ln, scale=1.0 / D, bias=EPS),
              reads=[sm.k], writes=[sm.k])
        P.act(lambda e: e.activation(out=sm[:, 2:3], in_=sm[:, 1:2], func=AF.Exp, scale=-0.5),
              reads=[sm.k], writes=[sm.k])
        P.dve(lambda e: e.tensor_scalar(out=xn[:], in0=xsrc, scalar1=sm[:, 2:3], scalar2=None, op0=ALU.mult),
              reads=[xb.k, sm.k], writes=[xn.k])
        ps = psb.next()
        for kc in range(8):
            P.pe(lambda e, kc=kc: e.transpose(out=ps[:, kc * 128:(kc + 1) * 128], in_=xn[:, kc * 128:(kc + 1) * 128],
                                              identity=ident_b[:]),
                 reads=[xn.k, ident_b.k], writes=[ps.k])
        P.dve(lambda e: e.tensor_tensor(out=dstT[:, :, c * 128:(c + 1) * 128],
                                        in0=ps[:, :].rearrange("p (k t) -> p k t", k=8),
                                        in1=pp[:, gain_off:gain_off + 8].unsqueeze(2).to_broadcast([128, 8, 128]),
                                        op=ALU.mult),
              reads=[ps.k, pp.k], writes=[dstT.k])
        return sm

    def load_panel(wb, K, f0, ncols, r0=0):
        wbp, tk = get_panel(wb, K, f0, ncols, r0)
        w = wring.next()
        view = w[:, 0:K * ncols].rearrange("p (k f) -> p k f", k=K)
        P.dma(lambda e: e.dma_start(out=w[:, 0:K * ncols], in_=wbp.ap()), reads=[tk], writes=[w.k])
        return w, view

    def proj_fm(wb, K, f0, nfc, rhs, evac):
        per = max(1, min(4, W_EL // (K * 128)))
        fc = 0
        while fc < nfc:
            n = min(per, nfc - fc)
            w, view = load_panel(wb, K, f0 + fc * 128, n * 128)
            for j in range(n):
                ps = psf.next()
                for kc in range(K):
                    P.pe(lambda e, kc=kc, j=j, ps=ps, view=view: e.matmul(
                        ps[:, 0:T], lhsT=view[:, kc, j * 128:(j + 1) * 128], rhs=rhs[:, kc, 0:T],
                        start=(kc == 0), stop=(kc == K - 1)),
                        reads=[w.k, rhs.k], writes=[ps.k])
                evac(fc + j, ps)
            fc += n

    def proj_tm(wb, K, f0, ncols, lhs, evac, panel_cols=None, ksplit=1):
        Kp = K // ksplit
        pc = panel_cols or min(512, (W_EL // Kp) // 128 * 128)
        col0 = 0
        while col0 < ncols:
            wd = min(pc, ncols - col0)
            pss = [psf.next() for _ in range(TC)]
            for kp in range(ksplit):
                w, view = load_panel(wb, Kp, f0 + col0, wd, r0=kp * Kp * 128)
                for c in range(TC):
                    ps = pss[c]
                    for kc in range(Kp):
                        kk = kp * Kp + kc
                        P.pe(lambda e, kc=kc, kk=kk, c=c, ps=ps, view=view, wd=wd: e.matmul(
                            ps[:, 0:wd], lhsT=lhs[:, kk, c * 128:(c + 1) * 128], rhs=view[:, kc, 0:wd],
                            start=(kk == 0), stop=(kk == K - 1)),
                            reads=[w.k, lhs.k], writes=[ps.k])
            for c in range(TC):
                evac(c, col0, wd, pss[c])
            col0 += wd

    def qkv_and_attention(g, b, hsrc, mt_index, blocks, au_rows, mid=None):
        rt = ropeT
        SK = cfg.get('skip', 0)
        if SK & 1:
            P.dve(lambda e: e.memset(rt[:], 1.0), writes=[rt.k])
        else:
            P.dma(lambda e: e.dma_start(out=rt[:], in_=rope_d.ap()[g, :, :, mt_index * T:(mt_index + 1) * T]
                                        .rearrange("f p t -> p f t")), writes=[rt.k])
        kTg = kT1
        Vg = Vb1

        def evac_qk(fc, ps):
            isk = fc >= 4
            h = fc % 4
            qraw = qraw_r.next()
            rtmp1 = rtmp1_r.next()
            rtmp2 = rtmp2_r.next()
            P.act(lambda e: e.activation(out=qraw[:], in_=ps[:, 0:T], func=AF.Copy), reads=[ps.k], writes=[qraw.k])
            if SK & 2:
                return
            ps2 = psf.next()
            P.pe(lambda e: e.matmul(ps2[:, 0:T], lhsT=rot_b[:], rhs=qraw[:], start=True, stop=True),
                 reads=[rot_b.k, qraw.k], writes=[ps2.k])
            if SK & 16:
                return
            P.dve(lambda e: e.tensor_tensor(out=rtmp1[:], in0=ps[:, 0:T], in1=rt[:, 2 * isk, :], op=ALU.mult),
                  reads=[ps.k, rt.k], writes=[rtmp1.k])
            P.dve(lambda e: e.tensor_tensor(out=rtmp2[:], in0=ps2[:, 0:T], in1=rt[:, 2 * isk + 1, :], op=ALU.mult),
                  reads=[ps2.k, rt.k], writes=[rtmp2.k])
            if SK & 32:
                return
            dst = kTg[:, h, 128:128 + T] if isk else qT[:, h, :]
            dk = kTg.k if isk else qT.k
            P.dve(lambda e: e.tensor_tensor(out=dst, in0=rtmp1[:], in1=rtmp2[:], op=ALU.add),
                  reads=[rtmp1.k, rtmp2.k], writes=[dk])

        proj_fm("win", 8, OFF_QKV + g * 512, 4, hsrc, lambda fc, ps: evac_qk(fc, ps))
        proj_fm("win", 8, OFF_QKV + 1536 + g * 512, 4, hsrc, lambda fc, ps: evac_qk(fc + 4, ps))

        def evac_v(c, col0, wd, ps):
            P.act(lambda e: e.activation(out=Vg[:, c + 1, :], in_=ps[:, 0:512], func=AF.Copy),
                  reads=[ps.k], writes=[Vg.k])
        if not SK & 4:
            proj_tm("win", 8, OFF_QKV + 3072 + g * 512, 512, hsrc, evac_v)

        if mid is not None:
            mid()
        AST = cfg.get('astage', 99)
        stages = []
        for j in range(TC if AST >= 3 else 0):
            hp = blocks[j]
            st = austage.next()
            k0 = 0 if hp else 128
            nk = 256 - k0
            nkc = nk // 128
            psS = [psf.next(), psf.next()]
            for h in range(4):
                bnk = psS[h // 2]
                o = (h % 2) * 256
                P.pe(lambda e, h=h, bnk=bnk, o=o: e.matmul(
                    bnk[:, o:o + nk], lhsT=qT[:, h, j * 128:(j + 1) * 128],
                    rhs=kTg[:, h, j * 128 + k0:j * 128 + 256], start=True, stop=False),
                    reads=[qT.k, kTg.k], writes=[bnk.k])
                P.pe(lambda e, bnk=bnk, o=o: e.matmul(
                    bnk[:, o:o + nk], lhsT=ident_b[:], rhs=maskb_b[:, k0:256], start=False, stop=True),
                    reads=[ident_b.k, maskb_b.k], writes=[bnk.k])
            if AST < 4:
                continue
            for pr in range(2):
                bnk = psS[pr]
                P.dve(lambda e, bnk=bnk, pr=pr: e.tensor_reduce(
                    out=st[:, 516 + 2 * pr:518 + 2 * pr], in_=bnk[:, :].rearrange("p (h k) -> p h k", h=2)[:, :, 0:nk],
                    axis=AX.X, op=ALU.max, negate=True),
                    reads=[bnk.k], writes=[st.k])
            for h in range(4):
                bnk = psS[h // 2]
                o = (h % 2) * 256
                P.act(lambda e, bnk=bnk, o=o, h=h: e.activation(
                    out=Pm4[:, h, 0:nk], in_=bnk[:, o:o + nk], func=AF.Exp, bias=st[:, 516 + h:517 + h],
                    accum_out=st[:, 512 + h:513 + h]),
                    reads=[bnk.k, st.k], writes=[Pm4.k, st.k])
            pT = psb.next()
            for h in range(4):
                for kc in range(nkc):
                    P.pe(lambda e, kc=kc, h=h: e.transpose(out=pT[:, (h * 2 + kc) * 128:(h * 2 + kc + 1) * 128],
                                                           in_=Pm4[:, h, kc * 128:(kc + 1) * 128], identity=ident_b[:]),
                         reads=[Pm4.k, ident_b.k], writes=[pT.k])
            P.dve(lambda e: e.tensor_copy(out=PT4[:, :, 0:nkc, :],
                                          in_=pT[:, :].rearrange("p (h c t) -> p h c t", h=4, c=2)[:, :, 0:nkc, :]),
                  reads=[pT.k], writes=[PT4.k])
            psU = psf.next()
            for h in range(4):
                for kc in range(nkc):
                    slot = j + kc if hp else j + 1
                    P.pe(lambda e, kc=kc, h=h, slot=slot: e.matmul(
                        psU[:, h * 128:(h + 1) * 128], lhsT=PT4[:, h, kc, :], rhs=Vg[:, slot, h * 128:(h + 1) * 128],
                        start=(kc == 0), stop=(kc == nkc - 1)),
                        reads=[PT4.k, Vg.k], writes=[psU.k])
            P.act(lambda e, psU=psU, st=st: e.activation(out=st[:, 0:512], in_=psU[:, 0:512], func=AF.Copy),
                  reads=[psU.k], writes=[st.k])
            dst = au_rows(j)
            if dst is not None:
                P.dma(lambda e, dst=dst, st=st: e.dma_start(out=dst, in_=st[:]), reads=[st.k], q='pool')
            stages.append(st)
        if SK & 8:
            return stages
        P.dve(lambda e: e.tensor_copy(out=kTg[:, :, 0:128], in_=kTg[:, :, T:T + 128]), reads=[kTg.k], writes=[kTg.k])
        P.dve(lambda e: e.tensor_copy(out=Vg[:, 0, :], in_=Vg[:, TC, :]), reads=[Vg.k], writes=[Vg.k])
        return stages

    out_ops = []
    for b in range(cfg.get('nb', NB)):
        perm_tiles = []
        for g in cfg.get('perm_groups', (2, 1)):
            L = S // PATTERNS[g][1]
            for mt in range(cfg.get('nmt_perm', NMT)):
                blks = [((mt * T + c * 128) // L, ((mt * T + c * 128) % L) // 128) for c in range(TC)]
                perm_tiles.append((g, mt, blks))

        def load_perm(desc):
            g, mt, blks = desc
            xv = x_d.ap()[b].rearrange("(j r) f -> r j f", r=PATTERNS[g][1])
            for c, (rho, n) in enumerate(blks):
                P.dma(lambda e: e.dma_start(out=xbuf[:, c, :], in_=xv[rho, n * 128:(n + 1) * 128, :]),
                      writes=[xbuf.k], q='pool')

        def load_main(mt, xb):
            for c in range(TC):
                P.dma(lambda e: e.dma_start(out=xb[:, c, :], in_=x_d.ap()[b, mt * T + c * 128:mt * T + (c + 1) * 128, :]),
                      writes=[xb.k], q='pool')

        main_prefetched = False
        if perm_tiles:
            load_perm(perm_tiles[0])
            for c in range(TC):
                rms_to_T(xbuf, hT, PP_NMIX, c)
            if len(perm_tiles) > 1:
                load_perm(perm_tiles[1])
        for i, (g, mt, blks) in enumerate(perm_tiles):
            def mid(i=i):
                if i + 1 < len(perm_tiles):
                    for c in range(TC):
                        rms_to_T(xbuf, hT, PP_NMIX, c)
                    if i + 2 < len(perm_tiles):
                        load_perm(perm_tiles[i + 2])
                    elif cfg.get('nmt', NMT) > 0:
                        load_main(0, xbuf)
            auv = au_d.ap()[b, g - 1].rearrange("(j r) f -> r j f", r=PATTERNS[g][1])
            qkv_and_attention(g, b, hT, mt, [n > 0 for (_, n) in blks],
                              lambda j: auv[blks[j][0], blks[j][1] * 128:(blks[j][1] + 1) * 128, :], mid=mid)
        if perm_tiles and cfg.get('nmt', NMT) > 0:
            if len(perm_tiles) == 1:
                load_main(0, xbuf)
            main_prefetched = True

        rms_done = False
        P.pool(lambda e: e.memset(S32[:], 0.0), writes=[S32.k])
        P.pool(lambda e: e.memset(Sbf[:], 0.0), writes=[Sbf.k])
        P.pool(lambda e: e.memset(halo[:], 0.0), writes=[halo.k])
        for mt in range(cfg.get('nmt', NMT)):
            t0 = mt * T
            if not main_prefetched:
                load_main(mt, xbuf)
            main_prefetched = False
            if not rms_done:
                for c in range(TC):
                    rms_to_T(xbuf, hT, PP_NMIX, c)
            rms_done = False
            if STAGE < 1:
                continue
            def evac_z(c, col0, wd, ps):
                P.act(lambda e: e.activation(out=zs[:, c, col0:col0 + wd], in_=ps[:, 0:wd], func=AF.Silu),
                      reads=[ps.k], writes=[zs.k])
            proj_tm("win", 8, OFF_Z, DI, hT, evac_z)

            P.pool(lambda e: e.tensor_copy(out=xbcT[:, :, 0:3], in_=halo[:]), reads=[halo.k], writes=[xbcT.k])

            def evac_xbc(fc, ps):
                P.act(lambda e: e.activation(out=xbcT[:, fc, 3:3 + T], in_=ps[:, 0:T], func=AF.Copy),
                      reads=[ps.k], writes=[xbcT.k])
            proj_fm("win", 8, OFF_XBC, 24, hT, evac_xbc)
            P.pool(lambda e: e.tensor_copy(out=halo[:], in_=xbcT[:, :, T:T + 3]), reads=[xbcT.k], writes=[halo.k])

            dts = []

            def evac_dt(c, col0, wd, ps):
                sm = small.next()
                P.dve(lambda e: e.tensor_tensor(out=sm[:, 0:32], in0=ps[:, 0:32], in1=bc[:, 0:32], op=ALU.add),
                      reads=[ps.k, bc.k], writes=[sm.k])
                P.act(lambda e: e.activation(out=sm[:, 0:32], in_=sm[:, 0:32], func=AF.Exp), reads=[sm.k], writes=[sm.k])
                P.act(lambda e: e.activation(out=sm[:, 0:32], in_=sm[:, 0:32], func=AF.Ln, bias=1.0),
                      reads=[sm.k], writes=[sm.k])
                P.dve(lambda e: e.tensor_tensor(out=sm[:, 32:64], in0=sm[:, 0:32], in1=A_bc[:], op=ALU.mult),
                      reads=[sm.k, A_bc.k], writes=[sm.k])
                dts.append(sm)
            proj_tm("win", 8, OFF_DT, 32, hT, evac_dt, panel_cols=32)

            if STAGE < 2:
                continue
            st1 = qkv_and_attention(0, b, hT, mt, [(mt * TC + c) > 0 for c in range(TC)], lambda j: None)

            if STAGE < 3:
                continue
            for i in range(8):
                cc = 16 + i
                ps = psf.next()
                for j in range(4):
                    P.pe(lambda e, j=j, cc=cc, ps=ps: e.matmul(ps[:, 0:T], lhsT=wdiag_b[:, j, cc, :],
                                                               rhs=xbcT[:, cc, j:j + T], start=(j == 0), stop=(j == 3)),
                         reads=[wdiag_b.k, xbcT.k], writes=[ps.k])
                dstb = BT if i < 4 else CT
                P.act(lambda e, ps=ps, cc=cc, dstb=dstb, i=i: e.activation(
                    out=dstb[:, i % 4, :], in_=ps[:, 0:T], func=AF.Silu, bias=pp[:, PP_CONVB + cc:PP_CONVB + cc + 1]),
                    reads=[ps.k, pp.k], writes=[dstb.k])

            def conv_tm(c):
                c0 = c * 128
                for blk in range(5):
                    ps = psf.next()
                    for q4 in range(4):
                        cc = blk * 4 + q4
                        for j in range(4):
                            P.pe(lambda e, j=j, cc=cc, ps=ps, q4=q4: e.matmul(
                                ps[:, q4 * 128:(q4 + 1) * 128], lhsT=xbcT[:, cc, c0 + j:c0 + j + 128],
                                rhs=wdiag_b[:, j, cc, :], start=(j == 0), stop=False),
                                reads=[wdiag_b.k, xbcT.k], writes=[ps.k])
                        P.pe(lambda e, cc=cc, ps=ps, q4=q4: e.matmul(
                            ps[:, q4 * 128:(q4 + 1) * 128], lhsT=ones_row_b[0:1, :],
                            rhs=convb_row_b[0:1, cc * 128:(cc + 1) * 128], start=False, stop=True),
                            reads=[ones_row_b.k, convb_row_b.k], writes=[ps.k])
                    if blk < 4:
                        P.act(lambda e, ps=ps, blk=blk: e.activation(out=xs_tok[:, blk * 512:(blk + 1) * 512],
                                                                     in_=ps[:, 0:512], func=AF.Silu),
                              reads=[ps.k], writes=[xs_tok.k])
                    else:
                        P.act(lambda e, ps=ps: e.activation(out=Btok[:], in_=ps[:, 0:512], func=AF.Silu),
                              reads=[ps.k], writes=[Btok.k])


            def scalars(c):
                sm = dts[c]
                c0 = c * 128
                sm2 = small.next()
                sm3 = small.next()
                psA = psf.next()
                P.pe(lambda e, psA=psA, sm=sm: e.matmul(psA[:, 0:32], lhsT=tri_f, rhs=sm[:, 32:64], start=True, stop=True),
                     reads=[mats_f.k, sm.k], writes=[psA.k])
                P.pe(lambda e, psA=psA, sm=sm: e.matmul(psA[:, 32:64], lhsT=ones_f, rhs=sm[:, 32:64], start=True, stop=True),
                     reads=[mats_f.k, sm.k], writes=[psA.k])
                P.dve(lambda e, psA=psA, sm2=sm2: e.tensor_copy(out=sm2[:, 0:32], in_=psA[:, 0:32]),
                      reads=[psA.k], writes=[sm2.k])
                P.act(lambda e, psA=psA, sm2=sm2: e.activation(out=sm2[:, 32:64], in_=psA[:, 0:32], func=AF.Exp),
                      reads=[psA.k], writes=[sm2.k])
                P.dve(lambda e, psA=psA, sm2=sm2, sm3=sm3: e.tensor_tensor(out=sm3[:, 0:32], in0=psA[:, 32:64],
                                                                          in1=sm2[:, 0:32], op=ALU.subtract),
                      reads=[psA.k, sm2.k], writes=[sm3.k])
                P.act(lambda e, sm3=sm3: e.activation(out=sm3[:, 0:32], in_=sm3[:, 0:32], func=AF.Exp),
                      reads=[sm3.k], writes=[sm3.k])
                P.act(lambda e, psA=psA, sm3=sm3: e.activation(out=sm3[:, 32:64], in_=psA[:, 32:64], func=AF.Exp),
                      reads=[psA.k], writes=[sm3.k])
                P.dve(lambda e, sm=sm: e.tensor_tensor(
                    out=xdt[:, :].rearrange("p (h d) -> p h d", h=32), in0=xs_tok[:, :].rearrange("p (h d) -> p h d", h=32),
                    in1=sm[:, 0:32].unsqueeze(2).to_broadcast([128, 32, 64]), op=ALU.mult),
                    reads=[xs_tok.k, sm.k], writes=[xdt.k])
                P.pool(lambda e: e.tensor_tensor(
                    out=ysum[:, :].rearrange("p (h d) -> p h d", h=32), in0=xs_tok[:, :].rearrange("p (h d) -> p h d", h=32),
                    in1=bc[:, 64:96].unsqueeze(2).to_broadcast([128, 32, 64]), op=ALU.mult),
                    reads=[xs_tok.k, bc.k], writes=[ysum.k])
                P.dve(lambda e, sm3=sm3: e.tensor_tensor(
                    out=xs_tok[:, :].rearrange("p (h d) -> p h d", h=32), in0=xdt[:, :].rearrange("p (h d) -> p h d", h=32),
                    in1=sm3[:, 0:32].unsqueeze(2).to_broadcast([128, 32, 64]), op=ALU.mult),
                    reads=[xdt.k, sm3.k], writes=[xs_tok.k])

                return sm2, sm3

            def au_load(c):
                tch = t0 + c * 128
                P.dma(lambda e, tch=tch: e.dma_start(out=au2[:], in_=au_d.ap()[b, 0, tch:tch + 128, :]), writes=[au2.k], q='pool')
                P.dma(lambda e, tch=tch: e.dma_start(out=au3[:], in_=au_d.ap()[b, 1, tch:tch + 128, :]), writes=[au3.k], q='pool')

            def group_front(c, gq):
                sm = dts[c]
                c0 = c * 128
                CBm = CBm_r.next()
                Rg = Rg_r.next()
                Eg = Eg_r.next()
                psC = psf.next()
                P.pe(lambda e: e.matmul(psC[:, 0:128], lhsT=BT[:, gq, c0:c0 + 128],
                                        rhs=CT[:, gq, c0:c0 + 128], start=True, stop=True),
                     reads=[BT.k, CT.k], writes=[psC.k])
                P.dve(lambda e: e.tensor_tensor(out=CBm[:], in0=psC[:, 0:128], in1=tri_f, op=ALU.mult),
                      reads=[psC.k, mats_f.k], writes=[CBm.k])
                P.pool(lambda e: e.tensor_tensor(
                    out=Rg[:], in0=sm[:, 32 + gq * 8:40 + gq * 8].unsqueeze(2).to_broadcast([128, 8, 128]),
                    in1=tri_f.unsqueeze(1).to_broadcast([128, 8, 128]), op=ALU.mult),
                    reads=[sm.k, mats_f.k], writes=[Rg.k])
                for hh in range(2):
                    psD = psf.next()
                    P.pe(lambda e, psD=psD, hh=hh: e.matmul(
                        psD[:, 0:512], lhsT=ustrict_f,
                        rhs=Rg[:, hh * 4:(hh + 1) * 4, :].rearrange("p h l -> p (h l)"), start=True, stop=True),
                        reads=[mats_f.k, Rg.k], writes=[psD.k])
                    P.act(lambda e, psD=psD, hh=hh: e.activation(
                        out=Eg[:, hh * 4:(hh + 1) * 4, :].rearrange("p h l -> p (h l)"), in_=psD[:, 0:512], func=AF.Exp),
                        reads=[psD.k], writes=[Eg.k])
                P.dve(lambda e: e.tensor_tensor(out=Eg[:], in0=Eg[:], in1=CBm[:, :].unsqueeze(1).to_broadcast([128, 8, 128]),
                                                op=ALU.mult),
                      reads=[Eg.k, CBm.k], writes=[Eg.k])
                return Eg

            def group_back(c, gq, Eg, sm2, sm3):
                c0 = c * 128
                ytmp = ytmp_r.next()
                psY = psf.next()
                for hl in range(8):
                    h = gq * 8 + hl
                    P.pe(lambda e, hl=hl, h=h: e.matmul(
                        psY[:, hl * 64:(hl + 1) * 64], lhsT=Eg[:, hl, :], rhs=xdt[:, h * 64:(h + 1) * 64],
                        start=True, stop=True),
                        reads=[Eg.k, xdt.k], writes=[psY.k])
                psO = psf.next()
                P.pe(lambda e: e.matmul(psO[:, 0:512], lhsT=CT[:, gq, c0:c0 + 128],
                                        rhs=Sbf[:, gq * 512:(gq + 1) * 512], start=True, stop=True),
                     reads=[CT.k, Sbf.k], writes=[psO.k])
                gs = slice(gq * 512, (gq + 1) * 512)
                P.dve(lambda e: e.tensor_tensor(
                    out=ytmp[:, :].rearrange("p (h d) -> p h d", h=8), in0=psO[:, 0:512].rearrange("p (h d) -> p h d", h=8),
                    in1=sm2[:, 32 + gq * 8:40 + gq * 8].unsqueeze(2).to_broadcast([128, 8, 64]), op=ALU.mult),
                    reads=[psO.k, sm2.k], writes=[ytmp.k])
                P.dve(lambda e: e.tensor_tensor(out=ysum[:, gs], in0=ysum[:, gs], in1=ytmp[:], op=ALU.add),
                      reads=[ysum.k, ytmp.k], writes=[ysum.k])
                P.dve(lambda e: e.tensor_tensor(out=ysum[:, gs], in0=psY[:, 0:512], in1=ysum[:, gs], op=ALU.add),
                      reads=[psY.k, ysum.k], writes=[ysum.k])
                psN = psf.next()
                P.pe(lambda e: e.matmul(psN[:, 0:512], lhsT=Btok[:, gq * 128:(gq + 1) * 128],
                                        rhs=xs_tok[:, gs], start=True, stop=True),
                     reads=[Btok.k, xs_tok.k], writes=[psN.k])
                P.dve(lambda e: e.tensor_tensor(
                    out=S32[:, gs].rearrange("p (h d) -> p h d", h=8), in0=S32[:, gs].rearrange("p (h d) -> p h d", h=8),
                    in1=sm3[:, 32 + gq * 8:40 + gq * 8].unsqueeze(2).to_broadcast([128, 8, 64]), op=ALU.mult),
                    reads=[S32.k, sm3.k], writes=[S32.k])
                P.dve(lambda e: e.tensor_tensor(out=S32[:, gs], in0=psN[:, 0:512], in1=S32[:, gs], op=ALU.add),
                      reads=[psN.k, S32.k], writes=[S32.k])
                P.act(lambda e: e.activation(out=Sbf[:, gs], in_=S32[:, gs], func=AF.Copy), reads=[S32.k], writes=[Sbf.k])

            def groups(c, sm2, sm3):
                fr = group_front(c, 0)
                for gq in range(4):
                    nxt = group_front(c, gq + 1) if gq < 3 else None
                    group_back(c, gq, fr, sm2, sm3)
                    fr = nxt

            def tail(c):
                c0 = c * 128
                P.dve(lambda e, c=c: e.tensor_tensor(out=ysum[:], in0=ysum[:], in1=zs[:, c, :], op=ALU.mult),
                      reads=[ysum.k, zs.k], writes=[ysum.k])
                sm4 = small.next()
                for gq in range(4):
                    P.act(lambda e, gq=gq, sm4=sm4: e.activation(out=ytmp_r.bufs[0][:],
                                                                 in_=ysum[:, gq * 512:(gq + 1) * 512], func=AF.Square,
                                                                 accum_out=sm4[:, gq:gq + 1]),
                          reads=[ysum.k], writes=[ytmp_r.bufs[0].k, sm4.k])
                P.act(lambda e, sm4=sm4: e.activation(out=sm4[:, 4:8], in_=sm4[:, 0:4], func=AF.Ln, scale=1.0 / 512, bias=EPS),
                      reads=[sm4.k], writes=[sm4.k])
                P.act(lambda e, sm4=sm4: e.activation(out=sm4[:, 8:12], in_=sm4[:, 4:8], func=AF.Exp, scale=-0.5),
                      reads=[sm4.k], writes=[sm4.k])
                for gq in range(4):
                    P.act(lambda e, gq=gq, sm4=sm4: e.activation(out=ynb[:, gq * 512:(gq + 1) * 512],
                                                                 in_=ysum[:, gq * 512:(gq + 1) * 512], func=AF.Copy,
                                                                 scale=sm4[:, 8 + gq:9 + gq]),
                          reads=[ysum.k, sm4.k], writes=[ynb.k])
                for half in range(2):
                    pT = psb.next()
                    for kc in range(8):
                        P.pe(lambda e, kc=kc, half=half, pT=pT: e.transpose(
                            out=pT[:, kc * 128:(kc + 1) * 128],
                            in_=ynb[:, (half * 8 + kc) * 128:(half * 8 + kc + 1) * 128], identity=ident_b[:]),
                            reads=[ynb.k, ident_b.k], writes=[pT.k])
                    P.dve(lambda e, half=half, pT=pT: e.tensor_tensor(
                        out=ynT[:, half * 8:(half + 1) * 8, c0:c0 + 128], in0=pT[:, :].rearrange("p (k t) -> p k t", k=8),
                        in1=pp[:, PP_SSMN + half * 8:PP_SSMN + half * 8 + 8].unsqueeze(2).to_broadcast([128, 8, 128]),
                        op=ALU.mult),
                        reads=[pT.k, pp.k], writes=[ynT.k])


            def merge(c):
                c0 = c * 128
                s1 = st1[c]
                sm5 = small.next()
                srcs = [s1, au2, au3]
                P.dve(lambda e, sm5=sm5, s1=s1: e.tensor_tensor(out=sm5[:, 0:4], in0=s1[:, 516:520], in1=au2[:, 516:520], op=ALU.min),
                      reads=[s1.k, au2.k], writes=[sm5.k])
                P.dve(lambda e, sm5=sm5: e.tensor_tensor(out=sm5[:, 0:4], in0=sm5[:, 0:4], in1=au3[:, 516:520], op=ALU.min),
                      reads=[sm5.k, au3.k], writes=[sm5.k])
                for gi, sr in enumerate(srcs):
                    P.dve(lambda e, sm5=sm5, sr=sr, gi=gi: e.tensor_tensor(out=sm5[:, 4 + gi * 4:8 + gi * 4], in0=sr[:, 516:520],
                                                                           in1=sm5[:, 0:4], op=ALU.subtract),
                          reads=[sr.k, sm5.k], writes=[sm5.k])
                P.act(lambda e, sm5=sm5: e.activation(out=sm5[:, 16:28], in_=sm5[:, 4:16], func=AF.Exp, scale=-1.0),
                      reads=[sm5.k], writes=[sm5.k])
                for gi, sr in enumerate(srcs):
                    P.dve(lambda e, sm5=sm5, sr=sr, gi=gi: e.tensor_tensor(out=sm5[:, 28 + gi * 4:32 + gi * 4], in0=sr[:, 512:516],
                                                                           in1=sm5[:, 16 + gi * 4:20 + gi * 4], op=ALU.mult),
                          reads=[sr.k, sm5.k], writes=[sm5.k])
                P.dve(lambda e, sm5=sm5: e.tensor_tensor(out=sm5[:, 40:44], in0=sm5[:, 28:32], in1=sm5[:, 32:36], op=ALU.add),
                      reads=[sm5.k], writes=[sm5.k])
                P.dve(lambda e, sm5=sm5: e.tensor_tensor(out=sm5[:, 40:44], in0=sm5[:, 40:44], in1=sm5[:, 36:40], op=ALU.add),
                      reads=[sm5.k], writes=[sm5.k])
                P.dve(lambda e, sm5=sm5: e.reciprocal(out=sm5[:, 44:48], in_=sm5[:, 40:44]), reads=[sm5.k], writes=[sm5.k])

                def fb(gi, sm5=sm5):
                    return sm5[:, 16 + gi * 4:20 + gi * 4].unsqueeze(2).to_broadcast([128, 4, 128])

                def v4(bf, lo=0):
                    return bf[:, lo:lo + 512].rearrange("p (h d) -> p h d", h=4)
                P.dve(lambda e, s1=s1: e.tensor_tensor(out=v4(num), in0=v4(s1), in1=fb(0), op=ALU.mult),
                      reads=[s1.k, sm5.k], writes=[num.k])
                P.pool(lambda e: e.tensor_tensor(out=v4(num2), in0=v4(au2), in1=fb(1), op=ALU.mult),
                       reads=[au2.k, sm5.k], writes=[num2.k])
                P.pool(lambda e: e.tensor_tensor(out=num[:], in0=num[:], in1=num2[:], op=ALU.add),
                       reads=[num.k, num2.k], writes=[num.k])
                P.dve(lambda e: e.tensor_tensor(out=v4(num2), in0=v4(au3), in1=fb(2), op=ALU.mult),
                      reads=[au3.k, sm5.k], writes=[num2.k])
                P.pool(lambda e: e.tensor_tensor(out=num[:], in0=num[:], in1=num2[:], op=ALU.add),
                       reads=[num.k, num2.k], writes=[num.k])
                P.dve(lambda e, sm5=sm5: e.tensor_tensor(out=v4(att), in0=v4(num),
                                                         in1=sm5[:, 44:48].unsqueeze(2).to_broadcast([128, 4, 128]), op=ALU.mult),
                      reads=[num.k, sm5.k], writes=[att.k])
                pT = psb.next()
                for kc in range(4):
                    P.pe(lambda e, kc=kc, pT=pT: e.transpose(out=pT[:, kc * 128:(kc + 1) * 128],
                                                             in_=att[:, kc * 128:(kc + 1) * 128], identity=ident_b[:]),
                         reads=[att.k, ident_b.k], writes=[pT.k])
                P.act(lambda e, pT=pT: e.activation(out=attT[:, :, c0:c0 + 128],
                                                    in_=pT[:, 0:512].rearrange("p (k t) -> p k t", k=4), func=AF.Copy),
                      reads=[pT.k], writes=[attT.k])


            conv_tm(0)
            ss = scalars(0)
            groups(0, *ss)
            for c in range(1, TC):
                conv_tm(c)
                au_load(c - 1)
                tail(c - 1)
                merge(c - 1)
                ss = scalars(c)
                groups(c, *ss)
            au_load(TC - 1)
            tail(TC - 1)
            merge(TC - 1)

            def evac_g(fc, ps):
                P.act(lambda e: e.activation(out=gT[:, fc, :], in_=ps[:, 0:T], func=AF.Sigmoid,
                                             bias=pp[:, PP_BG + fc:PP_BG + fc + 1]),
                      reads=[ps.k, pp.k], writes=[gT.k])
            proj_fm("win", 8, OFF_G, 16, hT, evac_g)


            if STAGE < 5:
                continue
            pend = {}

            def evac_ssm(fc, ps):
                P.dve(lambda e: e.tensor_tensor(out=ysum[:, fc * T:(fc + 1) * T], in0=ps[:, 0:T], in1=gT[:, fc, :], op=ALU.mult),
                      reads=[ps.k, gT.k], writes=[ysum.k])
            proj_fm("wso", 16, 0, 8, ynT, evac_ssm)

            def evac_att(fc, ps):
                P.dve(lambda e: e.tensor_tensor(out=mt2[:], in0=ps[:, 0:T], in1=gT[:, 8 + fc, :], op=ALU.mult),
                      reads=[ps.k, gT.k], writes=[mt2.k])
                P.pool(lambda e: e.tensor_tensor(out=mixedT[:, fc, :], in0=ysum[:, fc * T:(fc + 1) * T], in1=mt2[:], op=ALU.add),
                       reads=[mt2.k, ysum.k], writes=[mixedT.k])
            proj_fm("wao", 4, 0, 8, attT, evac_att)
            if mt + 1 < cfg.get('nmt', NMT):
                load_main(mt + 1, xview(ycur))
                main_prefetched = True

            def evac_mix(c, col0, wd, ps):
                P.dve(lambda e: e.tensor_tensor(out=xbuf[:, c, col0:col0 + wd], in0=ps[:, 0:wd], in1=xbuf[:, c, col0:col0 + wd],
                                                op=ALU.add),
                      reads=[ps.k, xbuf.k], writes=[xbuf.k])
            proj_tm("wmx", 8, 0, D, mixedT, evac_mix)

            if STAGE < 6:
                continue
            for c in range(TC):
                rms_to_T(xbuf, h2T, PP_NFFN, c)
            for fb0 in range(0, 22, 4):
                nfc = min(4, 22 - fb0)
                wg_, vg_ = load_panel("wg", 8, fb0 * 128, nfc * 128)
                wu_, vu_ = load_panel("wu", 8, fb0 * 128, nfc * 128)
                for j in range(nfc):
                    psg = psf.next()
                    psu = psf.next()
                    for kc in range(8):
                        P.pe(lambda e, kc=kc, j=j, psg=psg, vg_=vg_: e.matmul(psg[:, 0:T], lhsT=vg_[:, kc, j * 128:(j + 1) * 128],
                                                                             rhs=h2T[:, kc, :], start=(kc == 0), stop=(kc == 7)),
                             reads=[wg_.k, h2T.k], writes=[psg.k])
                    for kc in range(8):
                        P.pe(lambda e, kc=kc, j=j, psu=psu, vu_=vu_: e.matmul(psu[:, 0:T], lhsT=vu_[:, kc, j * 128:(j + 1) * 128],
                                                                             rhs=h2T[:, kc, :], start=(kc == 0), stop=(kc == 7)),
                             reads=[wu_.k, h2T.k], writes=[psu.k])
                    P.act(lambda e, psg=psg: e.activation(out=sg[:], in_=psg[:, 0:T], func=AF.Silu), reads=[psg.k], writes=[sg.k])
                    P.dve(lambda e, psu=psu, j=j, fb0=fb0: e.tensor_tensor(out=actT[:, fb0 + j, :], in0=psu[:, 0:T], in1=sg[:], op=ALU.mult),
                          reads=[psu.k, sg.k], writes=[actT.k])

            if main_prefetched:
                for c in range(TC):
                    rms_to_T(xview(ycur), hT, PP_NMIX, c)
                rms_done = True

            def evac_down(c, col0, wd, ps):
                P.dve(lambda e: e.tensor_tensor(out=xbuf[:, c, col0:col0 + wd], in0=ps[:, 0:wd], in1=xbuf[:, c, col0:col0 + wd],
                                                op=ALU.add),
                      reads=[ps.k, xbuf.k], writes=[xbuf.k])
            proj_tm("wd", 22, 0, D, actT, evac_down, panel_cols=256, ksplit=2)

            for c in range(TC):
                sm = small.next()
                P.act(lambda e, c=c, sm=sm: e.activation(out=xn[:], in_=xbuf[:, c, :], func=AF.Square, accum_out=sm[:, 0:1]),
                      reads=[xbuf.k], writes=[xn.k, sm.k])
                P.act(lambda e, sm=sm: e.activation(out=sm[:, 1:2], in_=sm[:, 0:1], func=AF.Ln, scale=1.0 / D, bias=EPS),
                      reads=[sm.k], writes=[sm.k])
                P.act(lambda e, sm=sm: e.activation(out=sm[:, 2:3], in_=sm[:, 1:2], func=AF.Exp, scale=-0.5),
                      reads=[sm.k], writes=[sm.k])
                P.dve(lambda e, c=c, sm=sm: e.scalar_tensor_tensor(out=xbuf[:, c, :], in0=xbuf[:, c, :], scalar=sm[:, 2:3], in1=nf[:],
                                                                   op0=ALU.mult, op1=ALU.mult),
                      reads=[xbuf.k, sm.k, nf.k], writes=[xbuf.k])
                o = P.dma(lambda e, c=c: e.dma_start(out=out_d.ap()[b, t0 + c * 128:t0 + (c + 1) * 128, :], in_=xbuf[:, c, :]),
                          reads=[xbuf.k], q='pool')
                out_ops.append(o)
            if main_prefetched:
                xcur, ycur = ycur, xcur
                xbuf = xview(xcur)
                ysum = yview(ycur)

    finals = [op for op in P.dma_last['pool'].values()]
    P.emit(final_waits=finals)
    nc._panel_order = list(panel_order)
    return nc


def _consts(inputs):
    f = np.float32
    i = np.arange(128)
    tri = (i[:, None] <= i[None, :]).astype(f)
    ustrict = (i[:, None] > i[None, :]).astype(f)
    ones = np.ones((128, 128), f)
    rot = np.zeros((128, 128), f)
    for dp in range(64):
        rot[dp + 64, dp] = -1.0
        rot[dp, dp + 64] = 1.0
    maskb = np.full((128, 256), NEG, f)
    q = i[:, None]
    kk = i[None, :]
    maskb[:, 0:128][kk >= q] = 0.0
    maskb[:, 128:256][kk <= q] = 0.0
    mats = np.concatenate([tri, ustrict, ones, rot, maskb], axis=1)
    conv_w = np.asarray(inputs["conv_w"], f)[0]
    wdiag = np.zeros((128, 4, 24, 128), f)
    for j in range(4):
        for cc in range(24):
            wdiag[i, j, cc, i] = conv_w[j, cc * 128:(cc + 1) * 128]
    conv_b = np.asarray(inputs["conv_b"], f)[0]
    pp = np.zeros((128, 72), f)
    pp[:, 0:24] = conv_b.reshape(24, 128).T
    pp[:, 24:40] = np.asarray(inputs["b_gate"], f)[0].reshape(16, 128).T
    pp[:, 40:56] = np.asarray(inputs["ssm_norm"], f)[0].reshape(16, 128).T
    pp[:, 56:64] = np.asarray(inputs["norm_mix"], f)[0].reshape(8, 128).T
    pp[:, 64:72] = np.asarray(inputs["norm_ffn"], f)[0].reshape(8, 128).T
    bcm = np.zeros((128, 96), f)
    bcm[:, 0:32] = np.asarray(inputs["dt_bias"], f)[0][None, :]
    bcm[:, 32:64] = np.asarray(inputs["a_log"], f)[0][None, :]
    bcm[:, 64:96] = np.asarray(inputs["d_skip"], f)[0][None, :]
    nfm = np.broadcast_to(np.asarray(inputs["norm_final"], f)[None, :], (128, D)).copy()
    half = 64
    inv = (10000.0 ** (-np.arange(half, dtype=np.float64) / half))
    pos = np.arange(S, dtype=np.float64)
    ang = pos[None, :] * np.concatenate([inv, inv])[:, None]
    cos, sin = np.cos(ang), np.sin(ang)
    rope = np.zeros((3, 4, 128, S), f)
    sc = 128.0 ** -0.5
    for g, (_, r) in enumerate(PATTERNS):
        L = S // r
        idx = np.arange(S)
        perm = (idx % L) * r + idx // L
        rope[g, 0] = cos[:, perm] * sc
        rope[g, 1] = sin[:, perm] * sc
        rope[g, 2] = cos[:, perm]
        rope[g, 3] = sin[:, perm]
    return dict(c_mats=mats, c_ident=np.eye(128, dtype=f), c_wdiag=wdiag.reshape(128, -1),
                c_convb_row=conv_b[None, :].copy(), c_pp=pp, c_bc=bcm, c_nf=nfm, rope=rope)


_NC_CACHE = {}


def kernel(**inputs):
    f = np.float32
    if "nc" not in _NC_CACHE:
        order = build_nc()._panel_order
        _NC_CACHE["nc"] = build_nc({"panel_order": order})
    nc = _NC_CACHE["nc"]
    x = np.ascontiguousarray(np.asarray(inputs["x"], f))
    shared = _consts(inputs)
    for k_in, k_dev in (("w_in", "w_in"), ("w_ssm_out", "w_ssm_out"), ("w_att_out", "w_att_out"),
                        ("w_mix_out", "w_mix_out"), ("w_ffn_gate", "w_ffn_gate"), ("w_ffn_up", "w_ffn_up"),
                        ("w_ffn_down", "w_ffn_down")):
        shared[k_dev] = np.ascontiguousarray(np.asarray(inputs[k_in], f)[0])
    in_maps = []
    for c in range(8):
        m = dict(shared)
        m["x"] = x[c * NB:(c + 1) * NB]
        in_maps.append(m)
    res = run_bass_kernel_spmd(nc, in_maps, core_ids=list(range(8)))
    return np.concatenate([r["out"] for r in res.results], axis=0).astype(f)
```

```python
import numpy as np
import concourse.bass as bass
import concourse.mybir as mybir
from concourse.bass_utils import run_bass_kernel_spmd

F32 = mybir.dt.float32
BF16 = mybir.dt.bfloat16
AF = mybir.ActivationFunctionType
ALU = mybir.AluOpType
AX = mybir.AxisListType

S = 4096
D = 1024
NB = 2
TC = 2
T = TC * 128
NMT = S // T
DI = 2048
CONVD = 3072
DFF = 2816
INP = 11808
OFF_Z, OFF_XBC, OFF_DT, OFF_QKV, OFF_G = 0, 2048, 5120, 5152, 9760
EPS = 1e-6
NEG = -30000.0
PATTERNS = ((128, 1), (512, 4), (2048, 16))
AUW = 520
W_EL = 4096
ENGS = ['pe', 'act', 'dve', 'pool', 'sp']


class Tok:
    __slots__ = ('name', 'w', 'r', 'excl')

    def __init__(self, name):
        self.name = name
        self.w = None
        self.r = []
        self.excl = False


class Op:
    __slots__ = ('eng', 'idx', 'fn', 'waits', 'signal', 'sigval', 'dma', 'sem', 'semval')

    def __init__(self, eng, idx, fn, dma):
        self.eng = eng
        self.idx = idx
        self.fn = fn
        self.waits = []
        self.signal = False
        self.sigval = None
        self.dma = dma
        self.sem = None
        self.semval = None


class Rec:
    def __init__(self):
        self.call = None

    def __getattr__(self, name):
        def f(*a, **k):
            self.call = (name, a, k)
            return self
        return f


class Prog:
    def __init__(self, nc):
        self.nc = nc
        self.ops = {e: [] for e in ENGS}
        self.seen = {e: {} for e in ENGS}
        self.seen_dma = {e: set() for e in ENGS}
        self.cnt = {e: nc.alloc_semaphore(name=f"cnt_{e}") for e in ['pe', 'act', 'dve', 'pool']}
        self.dma_pool = {q: [nc.alloc_semaphore(name=f"dma_{q}_{i}") for i in range(n)]
                         for q, n in (('sp', 12), ('act', 4), ('pool', 16))}
        self.dma_count = {q: 0 for q in self.dma_pool}
        self.dma_last = {q: {} for q in self.dma_pool}
        self.ntok = 0

    def tok(self, name=None):
        self.ntok += 1
        return Tok(name or f"t{self.ntok}")

    def add(self, eng, fn, reads=(), writes=(), dma=False):
        ops = self.ops[eng]
        rec = Rec()
        fn(rec)
        op = Op(eng, len(ops), rec.call, dma)
        deps = {}
        for t in reads:
            if t.w is not None:
                deps[id(t.w)] = (t.w, True)
            if t.excl:
                for r in t.r:
                    if r.eng != eng and id(r) not in deps:
                        deps[id(r)] = (r, False)
        for t in writes:
            if t.w is not None and id(t.w) not in deps:
                deps[id(t.w)] = (t.w, False)
            for r in t.r:
                if id(r) not in deps:
                    deps[id(r)] = (r, False)
        seen = self.seen[eng]
        best = {}
        for d, raw in deps.values():
            if d.dma:
                if id(d) in self.seen_dma[eng]:
                    continue
                self.seen_dma[eng].add(id(d))
                op.waits.append(d)
            else:
                if d.eng == eng and not dma:
                    if eng == 'pe' or not raw:
                        continue
                if seen.get(d.eng, -1) >= d.idx:
                    continue
                if d.eng not in best or best[d.eng].idx < d.idx:
                    best[d.eng] = d
        for d in best.values():
            seen[d.eng] = d.idx
            d.signal = True
            op.waits.append(d)
        if dma:
            q = eng
            k = self.dma_count[q]
            self.dma_count[q] = k + 1
            pool = self.dma_pool[q]
            si = k % len(pool)
            op.sem = pool[si]
            op.semval = 16 * (k // len(pool) + 1)
            prev = self.dma_last[q].get(si)
            if prev is not None and id(prev) not in self.seen_dma[eng]:
                self.seen_dma[eng].add(id(prev))
                op.waits.append(prev)
            self.dma_last[q][si] = op
        ops.append(op)
        for t in reads:
            t.r.append(op)
        for t in writes:
            t.w = op
            t.r = []
        return op

    def pe(self, fn, reads=(), writes=()):
        return self.add('pe', fn, reads, writes)

    def act(self, fn, reads=(), writes=()):
        return self.add('act', fn, reads, writes)

    def dve(self, fn, reads=(), writes=()):
        return self.add('dve', fn, reads, writes)

    def pool(self, fn, reads=(), writes=()):
        return self.add('pool', fn, reads, writes)

    def dma(self, fn, reads=(), writes=(), q='sp'):
        return self.add(q, fn, reads, writes, dma=True)

    def emit(self, final_waits=()):
        nc = self.nc
        for e in ['pe', 'act', 'dve', 'pool']:
            c = 0
            for op in self.ops[e]:
                if op.dma:
                    continue
                if op.signal:
                    c += 1
                    op.sigval = c
        cnt = self.cnt

        def run(engname, engobj, extra=()):
            for op in self.ops[engname]:
                for d in op.waits:
                    if d.dma:
                        engobj.wait_ge(d.sem, d.semval)
                    else:
                        engobj.wait_ge(cnt[d.eng], d.sigval)
                name, a, k = op.fn
                ins = getattr(engobj, name)(*a, **k)
                if op.dma:
                    ins.then_inc(op.sem, 16)
                elif op.signal:
                    ins.then_inc(cnt[op.eng], 1)
            for d in extra:
                engobj.wait_ge(d.sem, d.semval)

        with nc.Block() as block:
            @block.tensor
            def _(e):
                run('pe', e)

            @block.scalar
            def _(e):
                run('act', e)

            @block.vector
            def _(e):
                run('dve', e)

            @block.gpsimd
            def _(e):
                run('pool', e, extra=[d for d in final_waits if d.eng == 'pool'])

            @block.sync
            def _(e):
                run('sp', e, extra=[d for d in final_waits if d.eng == 'sp'])


class Buf:
    def __init__(self, P, t, name):
        self.t = t
        self.k = P.tok(name)

    def __getitem__(self, key):
        return self.t[key]


class Ring:
    def __init__(self, bufs):
        self.bufs = bufs
        self.i = 0

    def next(self):
        b = self.bufs[self.i % len(self.bufs)]
        self.i += 1
        return b


def build_nc(cfg=None):
    cfg = cfg or {}
    STAGE = cfg.get('stage', 99)
    nc = bass.Bass("TRN2", target_bir_lowering=False)
    P = Prog(nc)

    def dram_in(name, shape):
        return nc.dram_tensor(name, list(shape), F32, kind="ExternalInput")

    x_d = dram_in("x", [NB, S, D])
    win_d = dram_in("w_in", [D, INP])
    wso_d = dram_in("w_ssm_out", [DI, D])
    wao_d = dram_in("w_att_out", [512, D])
    wmx_d = dram_in("w_mix_out", [D, D])
    wg_d = dram_in("w_ffn_gate", [D, DFF])
    wu_d = dram_in("w_ffn_up", [D, DFF])
    wd_d = dram_in("w_ffn_down", [DFF, D])
    c_mats = dram_in("c_mats", [128, 4 * 128 + 256])
    c_ident = dram_in("c_ident", [128, 128])
    c_wdiag = dram_in("c_wdiag", [128, 4 * 24 * 128])
    c_convb_row = dram_in("c_convb_row", [1, CONVD])
    c_pp = dram_in("c_pp", [128, 72])
    c_bc = dram_in("c_bc", [128, 96])
    c_nf = dram_in("c_nf", [128, D])
    rope_d = dram_in("rope", [3, 4, 128, S])
    out_d = nc.dram_tensor("out", [NB, S, D], F32, kind="ExternalOutput")
    au_d = nc.dram_tensor("au", [NB, 2, S, AUW], F32, kind="Internal")

    def dram_bf(name, shape):
        return nc.dram_tensor(name, list(shape), BF16, kind="Internal")


    def sb(name, shape, dt=F32):
        return Buf(P, nc.alloc_sbuf_tensor(name, list(shape), dt), name)

    mats_f = sb("mats_f", [128, 4 * 128 + 256])
    ident_f = sb("ident_f", [128, 128])
    ident_b = sb("ident_b", [128, 128], BF16)
    rot_b = sb("rot_b", [128, 128], BF16)
    maskb_b = sb("maskb_b", [128, 256], BF16)
    wdiag_b = sb("wdiag_b", [128, 4, 24, 128], BF16)
    convb_row_b = sb("convb_row_b", [1, CONVD], BF16)
    ones_row_b = sb("ones_row_b", [1, 128], BF16)
    pp = sb("pp", [128, 72])
    bc = sb("bc", [128, 96])
    nf = sb("nf", [128, D])
    A_bc = sb("A_bc", [128, 32])
    tri_f = mats_f[:, 0:128]
    ustrict_f = mats_f[:, 128:256]
    ones_f = mats_f[:, 256:384]
    PP_CONVB, PP_BG, PP_SSMN, PP_NMIX, PP_NFFN = 0, 24, 40, 56, 64

    wsrc = {"win": win_d, "wso": wso_d, "wao": wao_d, "wmx": wmx_d, "wg": wg_d, "wu": wu_d, "wd": wd_d}
    panel_cache = {}
    panel_order = []

    def get_panel(name, K, f0, ncols, r0=0):
        key = (name, K, f0, ncols, r0)
        if key not in panel_cache:
            wbp = nc.dram_tensor(f"wp_{len(panel_cache)}", [128, K * ncols], BF16, kind="Internal")
            tk = P.tok(f"wp_{len(panel_cache)}")
            src = wsrc[name]
            P.dma(lambda e: e.dma_start(out=wbp.ap().rearrange("p (k f) -> p k f", k=K),
                                        in_=src.ap()[r0:r0 + K * 128, f0:f0 + ncols].rearrange("(k p) f -> p k f", p=128)),
                  writes=[tk], q='pool')
            panel_cache[key] = (wbp, tk)
            panel_order.append(key)
        return panel_cache[key]

    P.dma(lambda e: e.dma_start(out=mats_f[:], in_=c_mats.ap()), writes=[mats_f.k])
    P.dma(lambda e: e.dma_start(out=ident_f[:], in_=c_ident.ap()), writes=[ident_f.k])
    P.dma(lambda e: e.dma_start(out=pp[:], in_=c_pp.ap()), writes=[pp.k])
    P.dma(lambda e: e.dma_start(out=bc[:], in_=c_bc.ap()), writes=[bc.k])
    P.dma(lambda e: e.dma_start(out=nf[:], in_=c_nf.ap()), writes=[nf.k])
    if cfg.get('misc', 3) & 1:
        P.dma(lambda e: e.dma_start(out=wdiag_b[:], in_=c_wdiag.ap().rearrange("p (j c f) -> p j c f", j=4, c=24)),
              writes=[wdiag_b.k], q='pool')
    if cfg.get('misc', 3) & 2:
        P.dma(lambda e: e.dma_start(out=convb_row_b[:], in_=c_convb_row.ap()), writes=[convb_row_b.k], q='pool')
    P.dve(lambda e: e.tensor_copy(out=ident_b[:], in_=ident_f[:]), reads=[ident_f.k], writes=[ident_b.k])
    P.dve(lambda e: e.tensor_copy(out=rot_b[:], in_=mats_f[:, 384:512]), reads=[mats_f.k], writes=[rot_b.k])
    P.dve(lambda e: e.tensor_copy(out=maskb_b[:], in_=mats_f[:, 512:768]), reads=[mats_f.k], writes=[maskb_b.k])
    P.dve(lambda e: e.memset(ones_row_b[:], 1.0), writes=[ones_row_b.k])
    P.act(lambda e: e.activation(out=A_bc[:], in_=bc[:, 32:64], func=AF.Exp), reads=[bc.k], writes=[A_bc.k])
    P.act(lambda e: e.mul(out=A_bc[:], in_=A_bc[:], mul=-1.0), reads=[A_bc.k], writes=[A_bc.k])

    for key in (cfg.get('panel_order') or []):
        get_panel(*key)
    conv_ops = [op for op in P.ops['pool'] if op.dma]
    bar = P.add('sp', lambda e: e.nop())
    for op in conv_ops:
        bar.waits.append(op)
        P.seen_dma['sp'].add(id(op))

    bufA = sb("bufA", [128, TC * D])
    bufB = sb("bufB", [128, DI])
    xn = sb("xn", [128, D], BF16)
    hT = sb("hT", [128, 8, T], BF16)
    h2T = hT
    zs = sb("zs", [128, TC, DI], BF16)
    arena = sb("arena", [128, 24 * (T + 3)], BF16)

    class View:
        def __init__(self, ap, k):
            self.ap = ap
            self.k = k

        def __getitem__(self, key):
            return self.ap[key]
    xbcT = View(arena[:, :].rearrange("p (c t) -> p c t", c=24), arena.k)

    def xview(bf):
        return View(bf[:, :].rearrange("p (c d) -> p c d", c=TC), bf.k)

    def yview(bf):
        return View(bf[:, :], bf.k)
    xcur, ycur = bufA, bufB
    xbuf = xview(xcur)
    ysum = yview(ycur)
    actT = View(arena[:, 0:22 * T].rearrange("p (c t) -> p c t", c=22), arena.k)
    halo = sb("halo", [128, 24, 3], BF16)
    qraw_r = Ring([sb(f"qraw{i}", [128, T], BF16) for i in range(1)])
    rtmp1_r = Ring([sb(f"rtmp1_{i}", [128, T]) for i in range(2)])
    rtmp2_r = Ring([sb(f"rtmp2_{i}", [128, T]) for i in range(2)])
    mt2 = rtmp2_r.bufs[0]
    qT = sb("qT", [128, 4, T], BF16)
    kT1 = sb("kT1", [128, 4, 128 + T], BF16)
    Vb1 = sb("Vb1", [128, TC + 1, 512], BF16)
    Pm4 = sb("Pm4", [128, 4, 256], BF16)
    PT4 = sb("PT4", [128, 4, 2, 128], BF16)
    austage = Ring([sb(f"austage{i}", [128, AUW]) for i in range(2)])
    au3 = sb("au3", [128, AUW])
    xs_tok = sb("xs_tok", [128, DI], BF16)
    xdt = sb("xdt", [128, DI], BF16)
    ynb = xdt
    gT = View(zs[:, :, :].rearrange("p c (k t) -> p (c k) t", t=T), zs.k)
    mixedT = View(xdt[:, :].rearrange("p (k t) -> p k t", k=8), xdt.k)
    Btok = sb("Btok", [128, 512], BF16)
    BT = sb("BT", [128, 4, T], BF16)
    CT = sb("CT", [128, 4, T], BF16)
    Rg_r = Ring([sb(f"Rg{i}", [128, 8, 128]) for i in range(2)])
    Eg_r = Ring([sb(f"Eg{i}", [128, 8, 128], BF16) for i in range(2)])
    _rg0, _rg1, _eg0 = Rg_r.bufs[0], Rg_r.bufs[1], Eg_r.bufs[0]
    ropeT = View(_rg0[:, :, :].rearrange("p (f a) l -> p f (a l)", f=4), _rg0.k)
    au2 = View(_rg1[:, :, :].rearrange("p h l -> p (h l)")[:, 0:AUW], _rg1.k)
    att = View(_eg0[:, :, :].rearrange("p h l -> p (h l)")[:, 0:512], _eg0.k)
    CBm_r = Ring([sb(f"CBm{i}", [128, 128], BF16) for i in range(2)])
    S32 = sb("S32", [128, DI])
    Sbf = sb("Sbf", [128, DI], BF16)
    ytmp_r = Ring([sb(f"ytmp{i}", [128, 512]) for i in range(2)])
    num, num2 = ytmp_r.bufs[0], ytmp_r.bufs[1]
    ynT = sb("ynT", [128, 16, T], BF16)
    attT = sb("attT", [128, 4, T], BF16)
    sg = sb("sg", [128, T], BF16)
    wring = Ring([sb(f"wp{i}", [128, W_EL], BF16) for i in range(6)])
    small = Ring([sb(f"sm{i}", [128, 64]) for i in range(12)])
    psf = Ring([Buf(P, nc.alloc_psum_tensor(f"psf{i}", [128, 512], F32), f"psf{i}") for i in range(6)])
    psb = Ring([Buf(P, nc.alloc_psum_tensor(f"psb{i}", [128, 1024], BF16), f"psb{i}") for i in range(2)])
    for pb in psf.bufs + psb.bufs:
        pb.k.excl = True

    def rms_to_T(xb, dstT, gain_off, c):
        xsrc = xb[:, c, :]
        sm = small.next()
        P.act(lambda e: e.activation(out=xn[:], in_=xsrc, func=AF.Square, accum_out=sm[:, 0:1]),
              reads=[xb.k], writes=[xn.k, sm.k])
        P.act(lambda e: e.activation(out=sm[:, 1:2], in_=sm[:, 0:1], func=AF.Ln, scale=1.0 / D, bias=EPS),
              reads=[sm.k], writes=[sm.k])
        P.act(lambda e: e.activation(out=sm[:, 2:3], in_=sm[:, 1:2], func=AF.Exp, scale=-0.5),
              reads=[sm.k], writes=[sm.k])
        P.dve(lambda e: e.tensor_scalar(out=xn[:], in0=xsrc, scalar1=sm[:, 2:3], scalar2=None, op0=ALU.mult),
              reads=[xb.k, sm.k], writes=[xn.k])
        ps = psb.next()
        for kc in range(8):
            P.pe(lambda e, kc=kc: e.transpose(out=ps[:, kc * 128:(kc + 1) * 128], in_=xn[:, kc * 128:(kc + 1) * 128],
                                              identity=ident_b[:]),
                 reads=[xn.k, ident_b.k], writes=[ps.k])
        P.dve(lambda e: e.tensor_tensor(out=dstT[:, :, c * 128:(c + 1) * 128],
                                        in0=ps[:, :].rearrange("p (k t) -> p k t", k=8),
                                        in1=pp[:, gain_off:gain_off + 8].unsqueeze(2).to_broadcast([128, 8, 128]),
                                        op=ALU.mult),
              reads=[ps.k, pp.k], writes=[dstT.k])
        return sm

    def load_panel(wb, K, f0, ncols, r0=0):
        wbp, tk = get_panel(wb, K, f0, ncols, r0)
        w = wring.next()
        view = w[:, 0:K * ncols].rearrange("p (k f) -> p k f", k=K)
        P.dma(lambda e: e.dma_start(out=w[:, 0:K * ncols], in_=wbp.ap()), reads=[tk], writes=[w.k])
        return w, view

    def proj_fm(wb, K, f0, nfc, rhs, evac):
        per = max(1, min(4, W_EL // (K * 128)))
        fc = 0
        while fc < nfc:
            n = min(per, nfc - fc)
            w, view = load_panel(wb, K, f0 + fc * 128, n * 128)
            for j in range(n):
                ps = psf.next()
                for kc in range(K):
                    P.pe(lambda e, kc=kc, j=j, ps=ps, view=view: e.matmul(
                        ps[:, 0:T], lhsT=view[:, kc, j * 128:(j + 1) * 128], rhs=rhs[:, kc, 0:T],
                        start=(kc == 0), stop=(kc == K - 1)),
                        reads=[w.k, rhs.k], writes=[ps.k])
                evac(fc + j, ps)
            fc += n

    def proj_tm(wb, K, f0, ncols, lhs, evac, panel_cols=None, ksplit=1):
        Kp = K // ksplit
        pc = panel_cols or min(512, (W_EL // Kp) // 128 * 128)
        col0 = 0
        while col0 < ncols:
            wd = min(pc, ncols - col0)
            pss = [psf.next() for _ in range(TC)]
            for kp in range(ksplit):
                w, view = load_panel(wb, Kp, f0 + col0, wd, r0=kp * Kp * 128)
                for c in range(TC):
                    ps = pss[c]
                    for kc in range(Kp):
                        kk = kp * Kp + kc
                        P.pe(lambda e, kc=kc, kk=kk, c=c, ps=ps, view=view, wd=wd: e.matmul(
                            ps[:, 0:wd], lhsT=lhs[:, kk, c * 128:(c + 1) * 128], rhs=view[:, kc, 0:wd],
                            start=(kk == 0), stop=(kk == K - 1)),
                            reads=[w.k, lhs.k], writes=[ps.k])
            for c in range(TC):
                evac(c, col0, wd, pss[c])
            col0 += wd

    def qkv_and_attention(g, b, hsrc, mt_index, blocks, au_rows, mid=None):
        rt = ropeT
        SK = cfg.get('skip', 0)
        if SK & 1:
            P.dve(lambda e: e.memset(rt[:], 1.0), writes=[rt.k])
        else:
            P.dma(lambda e: e.dma_start(out=rt[:], in_=rope_d.ap()[g, :, :, mt_index * T:(mt_index + 1) * T]
                                        .rearrange("f p t -> p f t")), writes=[rt.k])
        kTg = kT1
        Vg = Vb1

        def evac_qk(fc, ps):
            isk = fc >= 4
            h = fc % 4
            qraw = qraw_r.next()
            rtmp1 = rtmp1_r.next()
            rtmp2 = rtmp2_r.next()
            P.act(lambda e: e.activation(out=qraw[:], in_=ps[:, 0:T], func=AF.Copy), reads=[ps.k], writes=[qraw.k])
            if SK & 2:
                return
            ps2 = psf.next()
            P.pe(lambda e: e.matmul(ps2[:, 0:T], lhsT=rot_b[:], rhs=qraw[:], start=True, stop=True),
                 reads=[rot_b.k, qraw.k], writes=[ps2.k])
            if SK & 16:
                return
            P.dve(lambda e: e.tensor_tensor(out=rtmp1[:], in0=ps[:, 0:T], in1=rt[:, 2 * isk, :], op=ALU.mult),
                  reads=[ps.k, rt.k], writes=[rtmp1.k])
            P.dve(lambda e: e.tensor_tensor(out=rtmp2[:], in0=ps2[:, 0:T], in1=rt[:, 2 * isk + 1, :], op=ALU.mult),
                  reads=[ps2.k, rt.k], writes=[rtmp2.k])
            if SK & 32:
                return
            dst = kTg[:, h, 128:128 + T] if isk else qT[:, h, :]
            dk = kTg.k if isk else qT.k
            P.dve(lambda e: e.tensor_tensor(out=dst, in0=rtmp1[:], in1=rtmp2[:], op=ALU.add),
                  reads=[rtmp1.k, rtmp2.k], writes=[dk])

        proj_fm("win", 8, OFF_QKV + g * 512, 4, hsrc, lambda fc, ps: evac_qk(fc, ps))
        proj_fm("win", 8, OFF_QKV + 1536 + g * 512, 4, hsrc, lambda fc, ps: evac_qk(fc + 4, ps))

        def evac_v(c, col0, wd, ps):
            P.act(lambda e: e.activation(out=Vg[:, c + 1, :], in_=ps[:, 0:512], func=AF.Copy),
                  reads=[ps.k], writes=[Vg.k])
        if not SK & 4:
            proj_tm("win", 8, OFF_QKV + 3072 + g * 512, 512, hsrc, evac_v)

        if mid is not None:
            mid()
        AST = cfg.get('astage', 99)
        stages = []
        for j in range(TC if AST >= 3 else 0):
            hp = blocks[j]
            st = austage.next()
            k0 = 0 if hp else 128
            nk = 256 - k0
            nkc = nk // 128
            psS = [psf.next(), psf.next()]
            for h in range(4):
                bnk = psS[h // 2]
                o = (h % 2) * 256
                P.pe(lambda e, h=h, bnk=bnk, o=o: e.matmul(
                    bnk[:, o:o + nk], lhsT=qT[:, h, j * 128:(j + 1) * 128],
                    rhs=kTg[:, h, j * 128 + k0:j * 128 + 256], start=True, stop=False),
                    reads=[qT.k, kTg.k], writes=[bnk.k])
                P.pe(lambda e, bnk=bnk, o=o: e.matmul(
                    bnk[:, o:o + nk], lhsT=ident_b[:], rhs=maskb_b[:, k0:256], start=False, stop=True),
                    reads=[ident_b.k, maskb_b.k], writes=[bnk.k])
            if AST < 4:
                continue
            for pr in range(2):
                bnk = psS[pr]
                P.dve(lambda e, bnk=bnk, pr=pr: e.tensor_reduce(
                    out=st[:, 516 + 2 * pr:518 + 2 * pr], in_=bnk[:, :].rearrange("p (h k) -> p h k", h=2)[:, :, 0:nk],
                    axis=AX.X, op=ALU.max, negate=True),
                    reads=[bnk.k], writes=[st.k])
            for h in range(4):
                bnk = psS[h // 2]
                o = (h % 2) * 256
                P.act(lambda e, bnk=bnk, o=o, h=h: e.activation(
                    out=Pm4[:, h, 0:nk], in_=bnk[:, o:o + nk], func=AF.Exp, bias=st[:, 516 + h:517 + h],
                    accum_out=st[:, 512 + h:513 + h]),
                    reads=[bnk.k, st.k], writes=[Pm4.k, st.k])
            pT = psb.next()
            for h in range(4):
                for kc in range(nkc):
                    P.pe(lambda e, kc=kc, h=h: e.transpose(out=pT[:, (h * 2 + kc) * 128:(h * 2 + kc + 1) * 128],
                                                           in_=Pm4[:, h, kc * 128:(kc + 1) * 128], identity=ident_b[:]),
                         reads=[Pm4.k, ident_b.k], writes=[pT.k])
            P.dve(lambda e: e.tensor_copy(out=PT4[:, :, 0:nkc, :],
                                          in_=pT[:, :].rearrange("p (h c t) -> p h c t", h=4, c=2)[:, :, 0:nkc, :]),
                  reads=[pT.k], writes=[PT4.k])
            psU = psf.next()
            for h in range(4):
                for kc in range(nkc):
                    slot = j + kc if hp else j + 1
                    P.pe(lambda e, kc=kc, h=h, slot=slot: e.matmul(
                        psU[:, h * 128:(h + 1) * 128], lhsT=PT4[:, h, kc, :], rhs=Vg[:, slot, h * 128:(h + 1) * 128],
                        start=(kc == 0), stop=(kc == nkc - 1)),
                        reads=[PT4.k, Vg.k], writes=[psU.k])
            P.act(lambda e, psU=psU, st=st: e.activation(out=st[:, 0:512], in_=psU[:, 0:512], func=AF.Copy),
                  reads=[psU.k], writes=[st.k])
            dst = au_rows(j)
            if dst is not None:
                P.dma(lambda e, dst=dst, st=st: e.dma_start(out=dst, in_=st[:]), reads=[st.k], q='pool')
            stages.append(st)
        if SK & 8:
            return stages
        P.dve(lambda e: e.tensor_copy(out=kTg[:, :, 0:128], in_=kTg[:, :, T:T + 128]), reads=[kTg.k], writes=[kTg.k])
        P.dve(lambda e: e.tensor_copy(out=Vg[:, 0, :], in_=Vg[:, TC, :]), reads=[Vg.k], writes=[Vg.k])
        return stages

    out_ops = []
    for b in range(cfg.get('nb', NB)):
        perm_tiles = []
        for g in cfg.get('perm_groups', (2, 1)):
            L = S // PATTERNS[g][1]
            for mt in range(cfg.get('nmt_perm', NMT)):
                blks = [((mt * T + c * 128) // L, ((mt * T + c * 128) % L) // 128) for c in range(TC)]
                perm_tiles.append((g, mt, blks))

        def load_perm(desc):
            g, mt, blks = desc
            xv = x_d.ap()[b].rearrange("(j r) f -> r j f", r=PATTERNS[g][1])
            for c, (rho, n) in enumerate(blks):
                P.dma(lambda e: e.dma_start(out=xbuf[:, c, :], in_=xv[rho, n * 128:(n + 1) * 128, :]),
                      writes=[xbuf.k], q='pool')

        def load_main(mt, xb):
            for c in range(TC):
                P.dma(lambda e: e.dma_start(out=xb[:, c, :], in_=x_d.ap()[b, mt * T + c * 128:mt * T + (c + 1) * 128, :]),
                      writes=[xb.k], q='pool')

        main_prefetched = False
        if perm_tiles:
            load_perm(perm_tiles[0])
            for c in range(TC):
                rms_to_T(xbuf, hT, PP_NMIX, c)
            if len(perm_tiles) > 1:
                load_perm(perm_tiles[1])
        for i, (g, mt, blks) in enumerate(perm_tiles):
            def mid(i=i):
                if i + 1 < len(perm_tiles):
                    for c in range(TC):
                        rms_to_T(xbuf, hT, PP_NMIX, c)
                    if i + 2 < len(perm_tiles):
                        load_perm(perm_tiles[i + 2])
                    elif cfg.get('nmt', NMT) > 0:
                        load_main(0, xbuf)
            auv = au_d.ap()[b, g - 1].rearrange("(j r) f -> r j f", r=PATTERNS[g][1])
            qkv_and_attention(g, b, hT, mt, [n > 0 for (_, n) in blks],
                              lambda j: auv[blks[j][0], blks[j][1] * 128:(blks[j][1] + 1) * 128, :], mid=mid)
        if perm_tiles and cfg.get('nmt', NMT) > 0:
            if len(perm_tiles) == 1:
                load_main(0, xbuf)
            main_prefetched = True

        rms_done = False
        P.pool(lambda e: e.memset(S32[:], 0.0), writes=[S32.k])
        P.pool(lambda e: e.memset(Sbf[:], 0.0), writes=[Sbf.k])
        P.pool(lambda e: e.memset(halo[:], 0.0), writes=[halo.k])
        for mt in range(cfg.get('nmt', NMT)):
            t0 = mt * T
            if not main_prefetched:
                load_main(mt, xbuf)
            main_prefetched = False
            if not rms_done:
                for c in range(TC):
                    rms_to_T(xbuf, hT, PP_NMIX, c)
            rms_done = False
            if STAGE < 1:
                continue
            def evac_z(c, col0, wd, ps):
                P.act(lambda e: e.activation(out=zs[:, c, col0:col0 + wd], in_=ps[:, 0:wd], func=AF.Silu),
                      reads=[ps.k], writes=[zs.k])
            proj_tm("win", 8, OFF_Z, DI, hT, evac_z)

            P.pool(lambda e: e.tensor_copy(out=xbcT[:, :, 0:3], in_=halo[:]), reads=[halo.k], writes=[xbcT.k])

            def evac_xbc(fc, ps):
                P.act(lambda e: e.activation(out=xbcT[:, fc, 3:3 + T], in_=ps[:, 0:T], func=AF.Copy),
                      reads=[ps.k], writes=[xbcT.k])
            proj_fm("win", 8, OFF_XBC, 24, hT, evac_xbc)
            P.pool(lambda e: e.tensor_copy(out=halo[:], in_=xbcT[:, :, T:T + 3]), reads=[xbcT.k], writes=[halo.k])

            dts = []

            def evac_dt(c, col0, wd, ps):
                sm = small.next()
                P.dve(lambda e: e.tensor_tensor(out=sm[:, 0:32], in0=ps[:, 0:32], in1=bc[:, 0:32], op=ALU.add),
                      reads=[ps.k, bc.k], writes=[sm.k])
                P.act(lambda e: e.activation(out=sm[:, 0:32], in_=sm[:, 0:32], func=AF.Exp), reads=[sm.k], writes=[sm.k])
                P.act(lambda e: e.activation(out=sm[:, 0:32], in_=sm[:, 0:32], func=AF.Ln, bias=1.0),
                      reads=[sm.k], writes=[sm.k])
                P.dve(lambda e: e.tensor_tensor(out=sm[:, 32:64], in0=sm[:, 0:32], in1=A_bc[:], op=ALU.mult),
                      reads=[sm.k, A_bc.k], writes=[sm.k])
                dts.append(sm)
            proj_tm("win", 8, OFF_DT, 32, hT, evac_dt, panel_cols=32)

            if STAGE < 2:
                continue
            st1 = qkv_and_attention(0, b, hT, mt, [(mt * TC + c) > 0 for c in range(TC)], lambda j: None)

            if STAGE < 3:
                continue
            for i in range(8):
                cc = 16 + i
                ps = psf.next()
                for j in range(4):
                    P.pe(lambda e, j=j, cc=cc, ps=ps: e.matmul(ps[:, 0:T], lhsT=wdiag_b[:, j, cc, :],
                                                               rhs=xbcT[:, cc, j:j + T], start=(j == 0), stop=(j == 3)),
                         reads=[wdiag_b.k, xbcT.k], writes=[ps.k])
                dstb = BT if i < 4 else CT
                P.act(lambda e, ps=ps, cc=cc, dstb=dstb, i=i: e.activation(
                    out=dstb[:, i % 4, :], in_=ps[:, 0:T], func=AF.Silu, bias=pp[:, PP_CONVB + cc:PP_CONVB + cc + 1]),
                    reads=[ps.k, pp.k], writes=[dstb.k])

            def conv_tm(c):
                c0 = c * 128
                for blk in range(5):
                    ps = psf.next()
                    P.pe(lambda e, ps=ps, blk=blk: e.matmul(
                        ps[:, 0:512], lhsT=ones_row_b[0:1, :],
                        rhs=convb_row_b[0:1, blk * 512:(blk + 1) * 512], start=True, stop=False),
                        reads=[ones_row_b.k, convb_row_b.k], writes=[ps.k])
                    for q4 in range(4):
                        cc = blk * 4 + q4
                        for j in range(4):
                            P.pe(lambda e, j=j, cc=cc, ps=ps, q4=q4: e.matmul(
                                ps[:, q4 * 128:(q4 + 1) * 128], lhsT=xbcT[:, cc, c0 + j:c0 + j + 128],
                                rhs=wdiag_b[:, j, cc, :], start=False, stop=(j == 3 and q4 == 3)),
                                reads=[wdiag_b.k, xbcT.k], writes=[ps.k])
                    if blk < 4:
                        P.act(lambda e, ps=ps, blk=blk: e.activation(out=xs_tok[:, blk * 512:(blk + 1) * 512],
                                                                     in_=ps[:, 0:512], func=AF.Silu),
                              reads=[ps.k], writes=[xs_tok.k])
                    else:
                        P.act(lambda e, ps=ps: e.activation(out=Btok[:], in_=ps[:, 0:512], func=AF.Silu),
                              reads=[ps.k], writes=[Btok.k])


            def scalars(c):
                sm = dts[c]
                c0 = c * 128
                sm2 = small.next()
                sm3 = small.next()
                psA = psf.next()
                P.pe(lambda e, psA=psA, sm=sm: e.matmul(psA[:, 0:32], lhsT=tri_f, rhs=sm[:, 32:64], start=True, stop=True),
                     reads=[mats_f.k, sm.k], writes=[psA.k])
                P.pe(lambda e, psA=psA, sm=sm: e.matmul(psA[:, 32:64], lhsT=ones_f, rhs=sm[:, 32:64], start=True, stop=True),
                     reads=[mats_f.k, sm.k], writes=[psA.k])
                P.dve(lambda e, psA=psA, sm2=sm2: e.tensor_copy(out=sm2[:, 0:32], in_=psA[:, 0:32]),
                      reads=[psA.k], writes=[sm2.k])
                P.act(lambda e, psA=psA, sm2=sm2: e.activation(out=sm2[:, 32:64], in_=psA[:, 0:32], func=AF.Exp),
                      reads=[psA.k], writes=[sm2.k])
                P.dve(lambda e, psA=psA, sm2=sm2, sm3=sm3: e.tensor_tensor(out=sm3[:, 0:32], in0=psA[:, 32:64],
                                                                          in1=sm2[:, 0:32], op=ALU.subtract),
                      reads=[psA.k, sm2.k], writes=[sm3.k])
                P.act(lambda e, sm3=sm3: e.activation(out=sm3[:, 0:32], in_=sm3[:, 0:32], func=AF.Exp),
                      reads=[sm3.k], writes=[sm3.k])
                P.act(lambda e, psA=psA, sm3=sm3: e.activation(out=sm3[:, 32:64], in_=psA[:, 32:64], func=AF.Exp),
                      reads=[psA.k], writes=[sm3.k])
                P.dve(lambda e, sm=sm: e.tensor_tensor(
                    out=xdt[:, :].rearrange("p (h d) -> p h d", h=32), in0=xs_tok[:, :].rearrange("p (h d) -> p h d", h=32),
                    in1=sm[:, 0:32].unsqueeze(2).to_broadcast([128, 32, 64]), op=ALU.mult),
                    reads=[xs_tok.k, sm.k], writes=[xdt.k])
                P.pool(lambda e: e.tensor_tensor(
                    out=ysum[:, :].rearrange("p (h d) -> p h d", h=32), in0=xs_tok[:, :].rearrange("p (h d) -> p h d", h=32),
                    in1=bc[:, 64:96].unsqueeze(2).to_broadcast([128, 32, 64]), op=ALU.mult),
                    reads=[xs_tok.k, bc.k], writes=[ysum.k])
                P.dve(lambda e, sm3=sm3: e.tensor_tensor(
                    out=xs_tok[:, :].rearrange("p (h d) -> p h d", h=32), in0=xdt[:, :].rearrange("p (h d) -> p h d", h=32),
                    in1=sm3[:, 0:32].unsqueeze(2).to_broadcast([128, 32, 64]), op=ALU.mult),
                    reads=[xdt.k, sm3.k], writes=[xs_tok.k])

                return sm2, sm3

            def au_load(c):
                tch = t0 + c * 128
                P.dma(lambda e, tch=tch: e.dma_start(out=au2[:], in_=au_d.ap()[b, 0, tch:tch + 128, :]), writes=[au2.k], q='pool')
                P.dma(lambda e, tch=tch: e.dma_start(out=au3[:], in_=au_d.ap()[b, 1, tch:tch + 128, :]), writes=[au3.k], q='pool')

            def group_front(c, gq):
                sm = dts[c]
                c0 = c * 128
                CBm = CBm_r.next()
                Rg = Rg_r.next()
                Eg = Eg_r.next()
                psC = psf.next()
                P.pe(lambda e: e.matmul(psC[:, 0:128], lhsT=BT[:, gq, c0:c0 + 128],
                                        rhs=CT[:, gq, c0:c0 + 128], start=True, stop=True),
                     reads=[BT.k, CT.k], writes=[psC.k])
                P.dve(lambda e: e.tensor_tensor(out=CBm[:], in0=psC[:, 0:128], in1=tri_f, op=ALU.mult),
                      reads=[psC.k, mats_f.k], writes=[CBm.k])
                P.pool(lambda e: e.tensor_tensor(
                    out=Rg[:], in0=sm[:, 32 + gq * 8:40 + gq * 8].unsqueeze(2).to_broadcast([128, 8, 128]),
                    in1=tri_f.unsqueeze(1).to_broadcast([128, 8, 128]), op=ALU.mult),
                    reads=[sm.k, mats_f.k], writes=[Rg.k])
                for hh in range(2):
                    psD = psf.next()
                    P.pe(lambda e, psD=psD, hh=hh: e.matmul(
                        psD[:, 0:512], lhsT=ustrict_f,
                        rhs=Rg[:, hh * 4:(hh + 1) * 4, :].rearrange("p h l -> p (h l)"), start=True, stop=True),
                        reads=[mats_f.k, Rg.k], writes=[psD.k])
                    P.act(lambda e, psD=psD, hh=hh: e.activation(
                        out=Eg[:, hh * 4:(hh + 1) * 4, :].rearrange("p h l -> p (h l)"), in_=psD[:, 0:512], func=AF.Exp),
                        reads=[psD.k], writes=[Eg.k])
                P.dve(lambda e: e.tensor_tensor(out=Eg[:], in0=Eg[:], in1=CBm[:, :].unsqueeze(1).to_broadcast([128, 8, 128]),
                                                op=ALU.mult),
                      reads=[Eg.k, CBm.k], writes=[Eg.k])
                return Eg

            def group_back(c, gq, Eg, sm2, sm3):
                c0 = c * 128
                ytmp = ytmp_r.next()
                psY = psf.next()
                for hl in range(8):
                    h = gq * 8 + hl
                    P.pe(lambda e, hl=hl, h=h: e.matmul(
                        psY[:, hl * 64:(hl + 1) * 64], lhsT=Eg[:, hl, :], rhs=xdt[:, h * 64:(h + 1) * 64],
                        start=True, stop=True),
                        reads=[Eg.k, xdt.k], writes=[psY.k])
                psO = psf.next()
                P.pe(lambda e: e.matmul(psO[:, 0:512], lhsT=CT[:, gq, c0:c0 + 128],
                                        rhs=Sbf[:, gq * 512:(gq + 1) * 512], start=True, stop=True),
                     reads=[CT.k, Sbf.k], writes=[psO.k])
                gs = slice(gq * 512, (gq + 1) * 512)
                P.dve(lambda e: e.tensor_tensor(
                    out=ytmp[:, :].rearrange("p (h d) -> p h d", h=8), in0=psO[:, 0:512].rearrange("p (h d) -> p h d", h=8),
                    in1=sm2[:, 32 + gq * 8:40 + gq * 8].unsqueeze(2).to_broadcast([128, 8, 64]), op=ALU.mult),
                    reads=[psO.k, sm2.k], writes=[ytmp.k])
                P.dve(lambda e: e.tensor_tensor(out=ysum[:, gs], in0=ysum[:, gs], in1=ytmp[:], op=ALU.add),
                      reads=[ysum.k, ytmp.k], writes=[ysum.k])
                P.dve(lambda e: e.tensor_tensor(out=ysum[:, gs], in0=psY[:, 0:512], in1=ysum[:, gs], op=ALU.add),
                      reads=[psY.k, ysum.k], writes=[ysum.k])
                psN = psf.next()
                P.pe(lambda e: e.matmul(psN[:, 0:512], lhsT=Btok[:, gq * 128:(gq + 1) * 128],
                                        rhs=xs_tok[:, gs], start=True, stop=True),
                     reads=[Btok.k, xs_tok.k], writes=[psN.k])
                P.dve(lambda e: e.tensor_tensor(
                    out=S32[:, gs].rearrange("p (h d) -> p h d", h=8), in0=S32[:, gs].rearrange("p (h d) -> p h d", h=8),
                    in1=sm3[:, 32 + gq * 8:40 + gq * 8].unsqueeze(2).to_broadcast([128, 8, 64]), op=ALU.mult),
                    reads=[S32.k, sm3.k], writes=[S32.k])
                P.dve(lambda e: e.tensor_tensor(out=S32[:, gs], in0=psN[:, 0:512], in1=S32[:, gs], op=ALU.add),
                      reads=[psN.k, S32.k], writes=[S32.k])
                P.act(lambda e: e.activation(out=Sbf[:, gs], in_=S32[:, gs], func=AF.Copy), reads=[S32.k], writes=[Sbf.k])

            def groups(c, sm2, sm3):
                fr = group_front(c, 0)
                for gq in range(4):
                    nxt = group_front(c, gq + 1) if gq < 3 else None
                    group_back(c, gq, fr, sm2, sm3)
                    fr = nxt

            def tail(c):
                c0 = c * 128
                P.dve(lambda e, c=c: e.tensor_tensor(out=ysum[:], in0=ysum[:], in1=zs[:, c, :], op=ALU.mult),
                      reads=[ysum.k, zs.k], writes=[ysum.k])
                sm4 = small.next()
                for gq in range(4):
                    P.act(lambda e, gq=gq, sm4=sm4: e.activation(out=ytmp_r.bufs[0][:],
                                                                 in_=ysum[:, gq * 512:(gq + 1) * 512], func=AF.Square,
                                                                 accum_out=sm4[:, gq:gq + 1]),
                          reads=[ysum.k], writes=[ytmp_r.bufs[0].k, sm4.k])
                P.act(lambda e, sm4=sm4: e.activation(out=sm4[:, 4:8], in_=sm4[:, 0:4], func=AF.Ln, scale=1.0 / 512, bias=EPS),
                      reads=[sm4.k], writes=[sm4.k])
                P.act(lambda e, sm4=sm4: e.activation(out=sm4[:, 8:12], in_=sm4[:, 4:8], func=AF.Exp, scale=-0.5),
                      reads=[sm4.k], writes=[sm4.k])
                for gq in range(4):
                    P.act(lambda e, gq=gq, sm4=sm4: e.activation(out=ynb[:, gq * 512:(gq + 1) * 512],
                                                                 in_=ysum[:, gq * 512:(gq + 1) * 512], func=AF.Copy,
                                                                 scale=sm4[:, 8 + gq:9 + gq]),
                          reads=[ysum.k, sm4.k], writes=[ynb.k])
                for half in range(2):
                    pT = psb.next()
                    for kc in range(8):
                        P.pe(lambda e, kc=kc, half=half, pT=pT: e.transpose(
                            out=pT[:, kc * 128:(kc + 1) * 128],
                            in_=ynb[:, (half * 8 + kc) * 128:(half * 8 + kc + 1) * 128], identity=ident_b[:]),
                            reads=[ynb.k, ident_b.k], writes=[pT.k])
                    P.dve(lambda e, half=half, pT=pT: e.tensor_tensor(
                        out=ynT[:, half * 8:(half + 1) * 8, c0:c0 + 128], in0=pT[:, :].rearrange("p (k t) -> p k t", k=8),
                        in1=pp[:, PP_SSMN + half * 8:PP_SSMN + half * 8 + 8].unsqueeze(2).to_broadcast([128, 8, 128]),
                        op=ALU.mult),
                        reads=[pT.k, pp.k], writes=[ynT.k])


            def merge(c):
                c0 = c * 128
                s1 = st1[c]
                sm5 = small.next()
                srcs = [s1, au2, au3]
                P.dve(lambda e, sm5=sm5, s1=s1: e.tensor_tensor(out=sm5[:, 0:4], in0=s1[:, 516:520], in1=au2[:, 516:520], op=ALU.min),
                      reads=[s1.k, au2.k], writes=[sm5.k])
                P.dve(lambda e, sm5=sm5: e.tensor_tensor(out=sm5[:, 0:4], in0=sm5[:, 0:4], in1=au3[:, 516:520], op=ALU.min),
                      reads=[sm5.k, au3.k], writes=[sm5.k])
                for gi, sr in enumerate(srcs):
                    P.dve(lambda e, sm5=sm5, sr=sr, gi=gi: e.tensor_tensor(out=sm5[:, 4 + gi * 4:8 + gi * 4], in0=sr[:, 516:520],
                                                                           in1=sm5[:, 0:4], op=ALU.subtract),
                          reads=[sr.k, sm5.k], writes=[sm5.k])
                P.act(lambda e, sm5=sm5: e.activation(out=sm5[:, 16:28], in_=sm5[:, 4:16], func=AF.Exp, scale=-1.0),
                      reads=[sm5.k], writes=[sm5.k])
                for gi, sr in enumerate(srcs):
                    P.dve(lambda e, sm5=sm5, sr=sr, gi=gi: e.tensor_tensor(out=sm5[:, 28 + gi * 4:32 + gi * 4], in0=sr[:, 512:516],
                                                                           in1=sm5[:, 16 + gi * 4:20 + gi * 4], op=ALU.mult),
                          reads=[sr.k, sm5.k], writes=[sm5.k])
                P.dve(lambda e, sm5=sm5: e.tensor_tensor(out=sm5[:, 40:44], in0=sm5[:, 28:32], in1=sm5[:, 32:36], op=ALU.add),
                      reads=[sm5.k], writes=[sm5.k])
                P.dve(lambda e, sm5=sm5: e.tensor_tensor(out=sm5[:, 40:44], in0=sm5[:, 40:44], in1=sm5[:, 36:40], op=ALU.add),
                      reads=[sm5.k], writes=[sm5.k])
                P.dve(lambda e, sm5=sm5: e.reciprocal(out=sm5[:, 44:48], in_=sm5[:, 40:44]), reads=[sm5.k], writes=[sm5.k])

                def fb(gi, sm5=sm5):
                    return sm5[:, 16 + gi * 4:20 + gi * 4].unsqueeze(2).to_broadcast([128, 4, 128])

                def v4(bf, lo=0):
                    return bf[:, lo:lo + 512].rearrange("p (h d) -> p h d", h=4)
                P.dve(lambda e, s1=s1: e.tensor_tensor(out=v4(num), in0=v4(s1), in1=fb(0), op=ALU.mult),
                      reads=[s1.k, sm5.k], writes=[num.k])
                P.pool(lambda e: e.tensor_tensor(out=v4(num2), in0=v4(au2), in1=fb(1), op=ALU.mult),
                       reads=[au2.k, sm5.k], writes=[num2.k])
                P.pool(lambda e: e.tensor_tensor(out=num[:], in0=num[:], in1=num2[:], op=ALU.add),
                       reads=[num.k, num2.k], writes=[num.k])
                P.dve(lambda e: e.tensor_tensor(out=v4(num2), in0=v4(au3), in1=fb(2), op=ALU.mult),
                      reads=[au3.k, sm5.k], writes=[num2.k])
                P.pool(lambda e: e.tensor_tensor(out=num[:], in0=num[:], in1=num2[:], op=ALU.add),
                       reads=[num.k, num2.k], writes=[num.k])
                P.dve(lambda e, sm5=sm5: e.tensor_tensor(out=v4(att), in0=v4(num),
                                                         in1=sm5[:, 44:48].unsqueeze(2).to_broadcast([128, 4, 128]), op=ALU.mult),
                      reads=[num.k, sm5.k], writes=[att.k])
                pT = psb.next()
                for kc in range(4):
                    P.pe(lambda e, kc=kc, pT=pT: e.transpose(out=pT[:, kc * 128:(kc + 1) * 128],
                                                             in_=att[:, kc * 128:(kc + 1) * 128], identity=ident_b[:]),
                         reads=[att.k, ident_b.k], writes=[pT.k])
                P.act(lambda e, pT=pT: e.activation(out=attT[:, :, c0:c0 + 128],
                                                    in_=pT[:, 0:512].rearrange("p (k t) -> p k t", k=4), func=AF.Copy),
                      reads=[pT.k], writes=[attT.k])


            conv_tm(0)
            ss = scalars(0)
            groups(0, *ss)
            for c in range(1, TC):
                conv_tm(c)
                au_load(c - 1)
                tail(c - 1)
                merge(c - 1)
                ss = scalars(c)
                groups(c, *ss)
            au_load(TC - 1)
            tail(TC - 1)
            merge(TC - 1)

            def evac_g(fc, ps):
                P.act(lambda e: e.activation(out=gT[:, fc, :], in_=ps[:, 0:T], func=AF.Sigmoid,
                                             bias=pp[:, PP_BG + fc:PP_BG + fc + 1]),
                      reads=[ps.k, pp.k], writes=[gT.k])
            proj_fm("win", 8, OFF_G, 16, hT, evac_g)


            if STAGE < 5:
                continue
            pend = {}

            def evac_ssm(fc, ps):
                P.dve(lambda e: e.tensor_tensor(out=ysum[:, fc * T:(fc + 1) * T], in0=ps[:, 0:T], in1=gT[:, fc, :], op=ALU.mult),
                      reads=[ps.k, gT.k], writes=[ysum.k])
            proj_fm("wso", 16, 0, 8, ynT, evac_ssm)

            def evac_att(fc, ps):
                P.dve(lambda e: e.tensor_tensor(out=mt2[:], in0=ps[:, 0:T], in1=gT[:, 8 + fc, :], op=ALU.mult),
                      reads=[ps.k, gT.k], writes=[mt2.k])
                P.pool(lambda e: e.tensor_tensor(out=mixedT[:, fc, :], in0=ysum[:, fc * T:(fc + 1) * T], in1=mt2[:], op=ALU.add),
                       reads=[mt2.k, ysum.k], writes=[mixedT.k])
            proj_fm("wao", 4, 0, 8, attT, evac_att)
            if mt + 1 < cfg.get('nmt', NMT):
                load_main(mt + 1, xview(ycur))
                main_prefetched = True

            def evac_mix(c, col0, wd, ps):
                P.dve(lambda e: e.tensor_tensor(out=xbuf[:, c, col0:col0 + wd], in0=ps[:, 0:wd], in1=xbuf[:, c, col0:col0 + wd],
                                                op=ALU.add),
                      reads=[ps.k, xbuf.k], writes=[xbuf.k])
            proj_tm("wmx", 8, 0, D, mixedT, evac_mix)

            if STAGE < 6:
                continue
            for c in range(TC):
                rms_to_T(xbuf, h2T, PP_NFFN, c)
            for fb0 in range(0, 22, 4):
                nfc = min(4, 22 - fb0)
                wg_, vg_ = load_panel("wg", 8, fb0 * 128, nfc * 128)
                wu_, vu_ = load_panel("wu", 8, fb0 * 128, nfc * 128)
                for j in range(nfc):
                    psg = psf.next()
                    psu = psf.next()
                    for kc in range(8):
                        P.pe(lambda e, kc=kc, j=j, psg=psg, vg_=vg_: e.matmul(psg[:, 0:T], lhsT=vg_[:, kc, j * 128:(j + 1) * 128],
                                                                             rhs=h2T[:, kc, :], start=(kc == 0), stop=(kc == 7)),
                             reads=[wg_.k, h2T.k], writes=[psg.k])
                    for kc in range(8):
                        P.pe(lambda e, kc=kc, j=j, psu=psu, vu_=vu_: e.matmul(psu[:, 0:T], lhsT=vu_[:, kc, j * 128:(j + 1) * 128],
                                                                             rhs=h2T[:, kc, :], start=(kc == 0), stop=(kc == 7)),
                             reads=[wu_.k, h2T.k], writes=[psu.k])
                    P.act(lambda e, psg=psg: e.activation(out=sg[:], in_=psg[:, 0:T], func=AF.Silu), reads=[psg.k], writes=[sg.k])
                    P.dve(lambda e, psu=psu, j=j, fb0=fb0: e.tensor_tensor(out=actT[:, fb0 + j, :], in0=psu[:, 0:T], in1=sg[:], op=ALU.mult),
                          reads=[psu.k, sg.k], writes=[actT.k])

            if main_prefetched:
                for c in range(TC):
                    rms_to_T(xview(ycur), hT, PP_NMIX, c)
                rms_done = True

            def evac_down(c, col0, wd, ps):
                P.dve(lambda e: e.tensor_tensor(out=xbuf[:, c, col0:col0 + wd], in0=ps[:, 0:wd], in1=xbuf[:, c, col0:col0 + wd],
                                                op=ALU.add),
                      reads=[ps.k, xbuf.k], writes=[xbuf.k])
            proj_tm("wd", 22, 0, D, actT, evac_down, panel_cols=256, ksplit=2)

            for c in range(TC):
                sm = small.next()
                P.act(lambda e, c=c, sm=sm: e.activation(out=xn[:], in_=xbuf[:, c, :], func=AF.Square, accum_out=sm[:, 0:1]),
                      reads=[xbuf.k], writes=[xn.k, sm.k])
                P.act(lambda e, sm=sm: e.activation(out=sm[:, 1:2], in_=sm[:, 0:1], func=AF.Ln, scale=1.0 / D, bias=EPS),
                      reads=[sm.k], writes=[sm.k])
                P.act(lambda e, sm=sm: e.activation(out=sm[:, 2:3], in_=sm[:, 1:2], func=AF.Exp, scale=-0.5),
                      reads=[sm.k], writes=[sm.k])
                P.dve(lambda e, c=c, sm=sm: e.scalar_tensor_tensor(out=xbuf[:, c, :], in0=xbuf[:, c, :], scalar=sm[:, 2:3], in1=nf[:],
                                                                   op0=ALU.mult, op1=ALU.mult),
                      reads=[xbuf.k, sm.k, nf.k], writes=[xbuf.k])
                o = P.dma(lambda e, c=c: e.dma_start(out=out_d.ap()[b, t0 + c * 128:t0 + (c + 1) * 128, :], in_=xbuf[:, c, :]),
                          reads=[xbuf.k], q='pool')
                out_ops.append(o)
            if main_prefetched:
                xcur, ycur = ycur, xcur
                xbuf = xview(xcur)
                ysum = yview(ycur)

    finals = [op for op in P.dma_last['pool'].values()]
    P.emit(final_waits=finals)
    nc._panel_order = list(panel_order)
    return nc


def _consts(inputs):
    f = np.float32
    i = np.arange(128)
    tri = (i[:, None] <= i[None, :]).astype(f)
    ustrict = (i[:, None] > i[None, :]).astype(f)
    ones = np.ones((128, 128), f)
    rot = np.zeros((128, 128), f)
    for dp in range(64):
        rot[dp + 64, dp] = -1.0
        rot[dp, dp + 64] = 1.0
    maskb = np.full((128, 256), NEG, f)
    q = i[:, None]
    kk = i[None, :]
    maskb[:, 0:128][kk >= q] = 0.0
    maskb[:, 128:256][kk <= q] = 0.0
    mats = np.concatenate([tri, ustrict, ones, rot, maskb], axis=1)
    conv_w = np.asarray(inputs["conv_w"], f)[0]
    wdiag = np.zeros((128, 4, 24, 128), f)
    for j in range(4):
        for cc in range(24):
            wdiag[i, j, cc, i] = conv_w[j, cc * 128:(cc + 1) * 128]
    conv_b = np.asarray(inputs["conv_b"], f)[0]
    pp = np.zeros((128, 72), f)
    pp[:, 0:24] = conv_b.reshape(24, 128).T
    pp[:, 24:40] = np.asarray(inputs["b_gate"], f)[0].reshape(16, 128).T
    pp[:, 40:56] = np.asarray(inputs["ssm_norm"], f)[0].reshape(16, 128).T
    pp[:, 56:64] = np.asarray(inputs["norm_mix"], f)[0].reshape(8, 128).T
    pp[:, 64:72] = np.asarray(inputs["norm_ffn"], f)[0].reshape(8, 128).T
    bcm = np.zeros((128, 96), f)
    bcm[:, 0:32] = np.asarray(inputs["dt_bias"], f)[0][None, :]
    bcm[:, 32:64] = np.asarray(inputs["a_log"], f)[0][None, :]
    bcm[:, 64:96] = np.asarray(inputs["d_skip"], f)[0][None, :]
    nfm = np.broadcast_to(np.asarray(inputs["norm_final"], f)[None, :], (128, D)).copy()
    half = 64
    inv = (10000.0 ** (-np.arange(half, dtype=np.float64) / half))
    pos = np.arange(S, dtype=np.float64)
    ang = pos[None, :] * np.concatenate([inv, inv])[:, None]
    cos, sin = np.cos(ang), np.sin(ang)
    rope = np.zeros((3, 4, 128, S), f)
    sc = 128.0 ** -0.5
    for g, (_, r) in enumerate(PATTERNS):
        L = S // r
        idx = np.arange(S)
        perm = (idx % L) * r + idx // L
        rope[g, 0] = cos[:, perm] * sc
        rope[g, 1] = sin[:, perm] * sc
        rope[g, 2] = cos[:, perm]
        rope[g, 3] = sin[:, perm]
    return dict(c_mats=mats, c_ident=np.eye(128, dtype=f), c_wdiag=wdiag.reshape(128, -1),
                c_convb_row=conv_b[None, :].copy(), c_pp=pp, c_bc=bcm, c_nf=nfm, rope=rope)


_NC_CACHE = {}


def kernel(**inputs):
    f = np.float32
    if "nc" not in _NC_CACHE:
        order = build_nc()._panel_order
        _NC_CACHE["nc"] = build_nc({"panel_order": order})
    nc = _NC_CACHE["nc"]
    x = np.ascontiguousarray(np.asarray(inputs["x"], f))
    shared = _consts(inputs)
    for k_in, k_dev in (("w_in", "w_in"), ("w_ssm_out", "w_ssm_out"), ("w_att_out", "w_att_out"),
                        ("w_mix_out", "w_mix_out"), ("w_ffn_gate", "w_ffn_gate"), ("w_ffn_up", "w_ffn_up"),
                        ("w_ffn_down", "w_ffn_down")):
        shared[k_dev] = np.ascontiguousarray(np.asarray(inputs[k_in], f)[0])
    in_maps = []
    for c in range(8):
        m = dict(shared)
        m["x"] = x[c * NB:(c + 1) * NB]
        in_maps.append(m)
    res = run_bass_kernel_spmd(nc, in_maps, core_ids=list(range(8)))
    return np.concatenate([r["out"] for r in res.results], axis=0).astype(f)
```
